# Optimizing a Trainium2 kernel written in Bass

```python
import math
import jax, jax.numpy as jnp
from jax import lax
import numpy as np

D_MODEL = 2048
BATCH = 8
SEQ = 4096
DEPTH = 2

N_MIXERS = 2
ATT_HEADS = 16
ATT_HEAD_DIM = D_MODEL // ATT_HEADS
DILATED_PATTERNS = ((128, 1), (512, 4), (2048, 16))
N_GROUPS = len(DILATED_PATTERNS)
REL_BUCKETS = 32
REL_MAX_DIST = 2048
RET_HEADS = D_MODEL // 256
RET_QK_DIM = D_MODEL
RET_V_DIM = 2 * D_MODEL
RET_HEAD_QK = RET_QK_DIM // RET_HEADS
RET_HEAD_V = RET_V_DIM // RET_HEADS
RET_CHUNK = 128
D_FF = ((8 * D_MODEL + 3 * 256 - 1) // (3 * 256)) * 256
NORM_EPS = 1e-6
MASK_VALUE = -1e30

kernel_name = "hybrid_dilated_attn_retention_block"


def rms_normalize(x):
    x32 = x.astype(jnp.float32)
    return x32 * lax.rsqrt(jnp.mean(x32 * x32, axis=-1, keepdims=True) + NORM_EPS)


def rms_norm(x, gain):
    return (rms_normalize(x) * gain.astype(jnp.float32)).astype(x.dtype)


def t5_bucket(dist):
    max_exact = REL_BUCKETS // 2
    d_f = jnp.maximum(dist, 1).astype(jnp.float32)
    large = max_exact + (jnp.log(d_f / max_exact) / math.log(REL_MAX_DIST / max_exact)
                         * (REL_BUCKETS - max_exact)).astype(jnp.int32)
    large = jnp.minimum(large, REL_BUCKETS - 1)
    return jnp.where(dist < max_exact, dist, large)


def dilated_group(q, k, v, window, dil, bias_table):
    B, S, H, Dh = q.shape
    steps = window // dil
    blk = steps
    span = dil * blk
    s_pad = -(-S // span) * span
    L = s_pad // dil
    nb = L // blk

    def to_blocks(t):
        t = jnp.pad(t, ((0, 0), (0, s_pad - S), (0, 0), (0, 0)))
        t = t.reshape(B, L, dil, H, Dh).transpose(0, 2, 1, 3, 4)
        return t.reshape(B, dil, nb, blk, H, Dh)

    def with_prev(t):
        prev = jnp.pad(t[:, :, :-1], ((0, 0), (0, 0), (1, 0), (0, 0), (0, 0), (0, 0)))
        return jnp.concatenate([prev, t], axis=3)

    qb = to_blocks(q)
    kk = with_prev(to_blocks(k))
    vv = with_prev(to_blocks(v))

    s = jnp.einsum("brnqhd,brnkhd->brnhqk", qb, kk).astype(jnp.float32) * (Dh ** -0.5)
    qi = jnp.arange(blk)[:, None]
    ki = jnp.arange(2 * blk)[None, :]
    delta = blk + qi - ki
    bucket = t5_bucket(jnp.maximum(delta, 0) * dil)
    bias = bias_table[bucket].transpose(2, 0, 1).astype(jnp.float32)
    valid = (delta >= 0) & (delta <= steps)
    first = (jnp.arange(nb) == 0)[:, None, None]
    valid = valid[None] & ~(first & (ki < blk)[None])
    s = jnp.where(valid[:, None], s + bias, MASK_VALUE)
    lse = jax.nn.logsumexp(s, axis=-1)
    p = jnp.exp(s - lse[..., None]).astype(v.dtype)
    o = jnp.einsum("brnhqk,brnkhd->brnqhd", p, vv)

    o = o.reshape(B, dil, L, H, Dh).transpose(0, 2, 1, 3, 4).reshape(B, s_pad, H, Dh)[:, :S]
    lse = lse.transpose(0, 1, 2, 4, 3).reshape(B, dil, L, H).transpose(0, 2, 1, 3)
    lse = lse.reshape(B, s_pad, H)[:, :S]
    return o, lse


def dilated_attention(h, w_qkv, q_gain, k_gain, w_o, rel_bias):
    B, S, _ = h.shape
    qkv = (h @ w_qkv).reshape(B, S, N_GROUPS, 3, ATT_HEADS, ATT_HEAD_DIM)
    outs, lses = [], []
    for g, (window, dil) in enumerate(DILATED_PATTERNS):
        q = rms_norm(qkv[:, :, g, 0], q_gain[g])
        k = rms_norm(qkv[:, :, g, 1], k_gain[g])
        v = qkv[:, :, g, 2]
        o, lse = dilated_group(q, k, v, window, dil,
                               rel_bias[:, g * ATT_HEADS:(g + 1) * ATT_HEADS])
        outs.append(o)
        lses.append(lse)
    weights = jax.nn.softmax(jnp.stack(lses, axis=-1), axis=-1)
    o = jnp.einsum("gbshd,bshg->bshd", jnp.stack(outs, axis=0), weights.astype(h.dtype))
    return o.reshape(B, S, D_MODEL) @ w_o


def retention_rotary(t):
    S = t.shape[1]
    half = t.shape[-1] // 2
    inv = 1.0 / (10000.0 ** jnp.linspace(0.0, 1.0, half, dtype=jnp.float32))
    ang = jnp.arange(S, dtype=jnp.float32)[:, None] * inv[None, :]
    cos = jnp.cos(ang)[None, :, None, :]
    sin = jnp.sin(ang)[None, :, None, :]
    t32 = t.astype(jnp.float32)
    t1, t2 = t32[..., :half], t32[..., half:]
    return jnp.concatenate([t1 * cos - t2 * sin, t1 * sin + t2 * cos], axis=-1)


def retention(h, w_qkvg, w_o):
    B, S, _ = h.shape
    q, k, v, g = jnp.split(h @ w_qkvg,
                           [RET_QK_DIM, 2 * RET_QK_DIM, 2 * RET_QK_DIM + RET_V_DIM], axis=-1)
    q = retention_rotary(q.reshape(B, S, RET_HEADS, RET_HEAD_QK))
    k = retention_rotary(k.reshape(B, S, RET_HEADS, RET_HEAD_QK)) * (RET_HEAD_QK ** -0.5)
    v = v.reshape(B, S, RET_HEADS, RET_HEAD_V).astype(jnp.float32)

    C = RET_CHUNK
    nc = S // C
    log_gamma = jnp.log(1.0 - 2.0 ** (-5.0 - jnp.arange(RET_HEADS, dtype=jnp.float32)))
    pos = jnp.arange(C, dtype=jnp.float32)
    diff = pos[:, None] - pos[None, :]
    inner_decay = jnp.where(diff[None] >= 0,
                            jnp.exp(jnp.maximum(diff, 0.0)[None] * log_gamma[:, None, None]),
                            0.0)
    cross_decay = jnp.exp((pos[:, None] + 1.0) * log_gamma[None, :])
    state_decay = jnp.exp((C - 1.0 - pos)[:, None] * log_gamma[None, :])
    chunk_decay = jnp.exp(C * log_gamma)

    def chunks(t):
        return jnp.moveaxis(t.reshape(B, nc, C, RET_HEADS, t.shape[-1]), 1, 0)

    def step(state, qkv_c):
        qc, kc, vc = qkv_c
        scores = jnp.einsum("bnhk,bmhk->bhnm", qc, kc) * inner_decay
        o = (jnp.einsum("bhnm,bmhv->bnhv", scores, vc)
             + jnp.einsum("bnhk,bhkv->bnhv", qc, state) * cross_decay[None, :, :, None])
        state = (state * chunk_decay[None, :, None, None]
                 + jnp.einsum("bmhk,bmhv->bhkv", kc * state_decay[None, :, :, None], vc))
        return state, o

    state0 = jnp.zeros((B, RET_HEADS, RET_HEAD_QK, RET_HEAD_V), jnp.float32)
    _, o = lax.scan(step, state0, (chunks(q), chunks(k), chunks(v)))
    o = jnp.moveaxis(o, 0, 1).reshape(B, S, RET_HEADS, RET_HEAD_V)
    o = rms_normalize(o).reshape(B, S, RET_V_DIM).astype(h.dtype)
    return (jax.nn.silu(g) * o) @ w_o


def swiglu(h, w_up, w_down):
    gate, up = jnp.split(h @ w_up, 2, axis=-1)
    return (jax.nn.silu(gate) * up) @ w_down


def setup_inputs(seed: int = 0) -> dict:
    key = jax.random.key(seed)
    ks = jax.random.split(key, 16)
    n_att = (DEPTH + N_MIXERS - 1) // N_MIXERS
    n_ret = DEPTH // N_MIXERS

    def nrm(k, shape, scale):
        return jax.random.normal(k, shape, jnp.float32) * scale

    return {
        "x": nrm(ks[0], (BATCH, SEQ, D_MODEL), 1.0),
        "c": nrm(ks[1], (BATCH, D_MODEL), 1.0),
        "w_mod": nrm(ks[2], (DEPTH, D_MODEL, 6 * D_MODEL), 0.5 * D_MODEL ** -0.5),
        "b_mod": nrm(ks[3], (DEPTH, 6 * D_MODEL), 0.02),
        "norm_mix": 1.0 + nrm(ks[4], (DEPTH, D_MODEL), 0.02),
        "norm_ffn": 1.0 + nrm(ks[5], (DEPTH, D_MODEL), 0.02),
        "rel_bias": nrm(ks[6], (REL_BUCKETS, N_GROUPS * ATT_HEADS), 0.5),
        "att_w_qkv": nrm(ks[7], (n_att, D_MODEL, N_GROUPS * 3 * ATT_HEADS * ATT_HEAD_DIM),
                          D_MODEL ** -0.5),
        "att_q_gain": 1.0 + nrm(ks[8], (n_att, N_GROUPS, ATT_HEAD_DIM), 0.02),
        "att_k_gain": 1.0 + nrm(ks[9], (n_att, N_GROUPS, ATT_HEAD_DIM), 0.02),
        "att_w_o": nrm(ks[10], (n_att, ATT_HEADS * ATT_HEAD_DIM, D_MODEL), D_MODEL ** -0.5),
        "ret_w_qkvg": nrm(ks[11], (n_ret, D_MODEL, 2 * RET_QK_DIM + 2 * RET_V_DIM),
                           D_MODEL ** -0.5),
        "ret_w_o": nrm(ks[12], (n_ret, RET_V_DIM, D_MODEL), RET_V_DIM ** -0.5),
        "ffn_w_up": nrm(ks[13], (DEPTH, D_MODEL, 2 * D_FF), D_MODEL ** -0.5),
        "ffn_w_down": nrm(ks[14], (DEPTH, D_FF, D_MODEL), D_FF ** -0.5),
    }


def reference(x, c, w_mod, b_mod, norm_mix, norm_ffn, rel_bias, att_w_qkv, att_q_gain,
              att_k_gain, att_w_o, ret_w_qkvg, ret_w_o, ffn_w_up, ffn_w_down):
    cond = jax.nn.silu(c)
    for i in range(DEPTH):
        mod = (cond @ w_mod[i] + b_mod[i])[:, None, :]
        sh1, sc1, g1, sh2, sc2, g2 = jnp.split(mod, 6, axis=-1)
        j = i // N_MIXERS
        h = rms_norm(x, norm_mix[i]) * (1.0 + sc1) + sh1
        if i % N_MIXERS == 0:
            y = dilated_attention(h, att_w_qkv[j], att_q_gain[j], att_k_gain[j], att_w_o[j],
                                  rel_bias)
        else:
            y = retention(h, ret_w_qkvg[j], ret_w_o[j])
        x = x + g1 * y
        h = rms_norm(x, norm_ffn[i]) * (1.0 + sc2) + sh2
        x = x + g2 * swiglu(h, ffn_w_up[i], ffn_w_down[i])
    return x
```

```python
import math
from contextlib import ExitStack

import numpy as np
import ml_dtypes
import concourse.bass as bass
import concourse.mybir as mybir
from concourse.bass_utils import run_bass_kernel_spmd

F32 = mybir.dt.float32
BF16 = mybir.dt.bfloat16
AF = mybir.ActivationFunctionType
ALU = mybir.AluOpType
ENG = ("sync", "scalar", "vector", "gpsimd", "tensor")

D = 2048
S = 4096
DC = D // 128
DFF = 5632
EPS = 1e-6
NEG = -30000.0
DILS = (1, 4, 16)
RH = 8
GAMMA = [1.0 - 2.0 ** (-5.0 - h) for h in range(RH)]


class Sem:
    def __init__(self, h, name):
        self.h = h
        self.n = 0
        self.name = name


class KB:
    def __init__(self, nc, stack):
        self.nc = nc
        self.stack = stack
        self.ops = {e: [] for e in ENG}
        self.nsem = 0
        self.free_sems = []
        self.phase_sems = []

    def sem(self, name):
        if self.free_sems:
            s = self.free_sems.pop()
        else:
            h = self.stack.enter_context(self.nc.semaphore(f"s{self.nsem}"))
            self.nsem += 1
            s = Sem(h, name)
        s.name = name
        self.phase_sems.append(s)
        return s

    def op(self, eng, fn, waits=(), inc=None, k=None):
        ws = []
        for w in waits:
            if w is None:
                continue
            s, v = w
            if v <= 0:
                continue
            assert v <= s.n, f"wait on future value {s.name}: {v} > {s.n}"
            ws.append((s.h, v))
        tok = None
        ins = None
        if inc is not None:
            if k is None:
                k = 1
            inc.n += k
            ins = (inc.h, k)
            tok = (inc, inc.n)
        self.ops[eng].append((ws, fn, ins))
        return tok

    def dma(self, eng, out, in_, waits=(), inc=None):
        return self.op(eng, lambda e: e.dma_start(out=out, in_=in_), waits=waits, inc=inc, k=16)

    def flush(self):
        ops = self.ops
        self.ops = {e: [] for e in ENG}
        with self.nc.Block() as block:
            for name in ENG:
                lst = ops[name]
                if not lst:
                    continue

                def body(e, lst=lst):
                    for ws, fn, ins in lst:
                        for h, v in ws:
                            e.wait_ge(h, v)
                        i = fn(e)
                        if ins is not None:
                            i.then_inc(ins[0], ins[1])

                getattr(block, name)(body)
        self.free_sems.extend(self.phase_sems)
        self.phase_sems = []


class Slots:
    def __init__(self, n):
        self.n = n
        self.use = 0
        self.rel = {}

    def next(self):
        u = self.use
        self.use += 1
        w = self.rel.get(u - self.n, [])
        return u, u % self.n, list(w)

    def release(self, u, *toks):
        self.rel.setdefault(u, []).extend([t for t in toks if t is not None])


def mm(out, lhsT, rhs, start=True, stop=True):
    return lambda e: e.matmul(out, lhsT, rhs, start=start, stop=stop)


class Prog:
    def __init__(self, nc, debug=None):
        self.nc = nc
        self.debug = debug or {}
        self.gstack = ExitStack()
        self.kb = KB(nc, self.gstack)

    def din(self, name, shape, dt=F32):
        return self.nc.dram_tensor(name, list(shape), dt, kind="ExternalInput").ap()

    def dscr(self, name, shape, dt):
        kind = "ExternalOutput" if name in self.debug.get("outs", ()) else "Internal"
        return self.nc.dram_tensor(name, list(shape), dt, kind=kind).ap()

    def gsb(self, name, shape, dt):
        return self.gstack.enter_context(self.nc.sbuf_tensor(name, list(shape), dt))

    def declare(self):
        d = self.din
        self.xT = d("xT", [D, S])
        self.cT = d("cT", [128, DC])
        self.w_mod = d("w_mod", [2, D, 6 * D])
        self.b_modT = d("b_modT", [2, 128, 96])
        self.nmixT = d("nmixT", [2, 128, DC])
        self.nffnT = d("nffnT", [2, 128, DC])
        self.relb = d("relb", [32, 48])
        self.w_qkv = d("w_qkv", [D, 9 * D])
        self.qgT = d("qgT", [128, 3])
        self.kgT = d("kgT", [128, 3])
        self.w_ao = d("w_ao", [D, D])
        self.w_rq = d("w_rq", [D, 6 * D])
        self.w_ro = d("w_ro", [2 * D, D])
        self.w_up = d("w_up", [2, D, 2 * DFF])
        self.w_dn = d("w_dn", [2, DFF, D])
        self.c_ident = d("c_ident", [128, 128], BF16)
        self.c_onehot = d("c_onehot", [33, 3, 384])
        self.c_cos = d("c_cos", [128, S])
        self.c_sin = d("c_sin", [128, S])
        self.c_decT = d("c_decT", [128, RH, 128])
        self.c_cdB = d("c_cdB", [128, RH, 128])
        self.c_sd = d("c_sd", [128, RH])
        self.c_rev = d("c_rev", [128, 128])
        oname = "outT"
        self.outT = self.nc.dram_tensor(oname, [D, S], F32, kind="ExternalOutput").ap()
        s = self.dscr
        self.hT = s("hT", [D, S], BF16)
        self.qTd = s("qTd", [3, 16, 128, S], BF16)
        self.kTd = s("kTd", [3, 16, 128, S], BF16)
        self.vd = s("vd", [3, 16, 128, 32, 128], BF16)
        self.vecd = s("vecd", [3, 16, 384], F32)
        self.oTd = s("oTd", [D, S], BF16)
        self.x1 = s("x1", [D, S], F32)
        self.x2 = s("x2", [D, S], F32)
        self.x3 = s("x3", [D, S], F32)
        self.hid = s("hid", [DFF, S], BF16)
        self.rqT = s("rqT", [D, S], BF16)
        self.rkT = s("rkT", [D, S], BF16)
        self.rv = s("rv", [RH, 32, 128, 512], BF16)
        self.rgT = s("rgT", [2 * D, S], BF16)
        self.roT = s("roT", [2 * D, S], BF16)
        self.par = self.gsb("par", [128, 2, 6, DC], F32)
        self.ident = self.gsb("ident", [128, 128], BF16)
        self.ones = self.gsb("ones", [128, 128], BF16)
        self.epsc = self.gsb("epsc", [128, 1], F32)
        self.gq = self.gsb("gq", [128, 2, 3], F32)
        self.condb = self.gsb("condb", [128, DC], BF16)
        self.misc = self.gsb("misc", [128, 2, 96 + 2 * DC], F32)
        self.modp = self.gsb("modp", [128, 2, 96], F32)
        self.tmp1 = self.gsb("tmp1", [128, 2, 2, DC], F32)
        self.psum = self.gstack.enter_context(self.nc.psum_tensor("ps", [128, 8, 512], F32))

    def phase_mod(self):
        kb, nc = self.kb, self.nc
        with ExitStack() as ps:
            sb = lambda n, sh, dt: ps.enter_context(nc.sbuf_tensor(n, list(sh), dt))
            cond = sb("cond", [128, DC], F32)
            misc, modp, tmp1 = self.misc, self.modp, self.tmp1
            NST = 3
            stage = [sb(f"mst{i}", [128, DC, 512], F32) for i in range(NST)]
            ld = kb.sem("mod_ld")
            kb.dma("sync", cond[:], self.cT[:, :], inc=ld)
            kb.dma("sync", self.ident[:], self.c_ident[:, :], inc=ld)
            kb.dma("sync", self.gq[:, 0, :], self.qgT[:, :], inc=ld)
            kb.dma("sync", self.gq[:, 1, :], self.kgT[:, :], inc=ld)
            for i in range(2):
                kb.dma("sync", misc[:, i, 0:96], self.b_modT[i], inc=ld)
                kb.dma("sync", misc[:, i, 96:96 + DC], self.nmixT[i], inc=ld)
                kb.dma("sync", misc[:, i, 96 + DC:96 + 2 * DC], self.nffnT[i], inc=ld)
            ldtok = (ld, ld.n)
            c_rdy = kb.sem("mod_c")
            kb.op("gpsimd", lambda e: e.memset(self.ones[:], 1.0), inc=c_rdy)
            kb.op("gpsimd", lambda e: e.memset(self.epsc[:], EPS), inc=c_rdy)
            kb.op("vector", lambda e: e.tensor_scalar(out=self.gq[:, 0, :], in0=self.gq[:, 0, :],
                                                     scalar1=128.0 ** -0.5, scalar2=None, op0=ALU.mult),
                  waits=[ldtok], inc=c_rdy)
            ctok = kb.op("scalar", lambda e: e.activation(out=cond[:], in_=cond[:], func=AF.Silu),
                         waits=[ldtok], inc=c_rdy)
            st_sl = Slots(NST)
            bf_sl = Slots(2)
            st_rdy = [kb.sem(f"mod_sr{i}") for i in range(NST)]
            pe = kb.sem("mod_pe")
            ev = kb.sem("mod_ev")
            mwb = [sb(f"mwb{i}", [128, DC, 512], BF16) for i in range(2)]
            condb = self.condb
            cbtok = kb.op("vector", lambda e: e.tensor_copy(out=condb[:], in_=cond[:]), waits=[ctok], inc=ev)
            for i in range(1):
                pst = self.psum[:, i, 0:96]
                for s_ in range(24):
                    u, slot, w = st_sl.next()
                    for h in range(2):
                        tok = kb.dma("sync", stage[slot][:, h * 8:(h + 1) * 8, :],
                                     self.w_mod[i, h * 1024:(h + 1) * 1024, s_ * 512:(s_ + 1) * 512]
                                     .rearrange("(c p) n -> p c n", p=128),
                                     waits=w if h == 0 else (), inc=st_rdy[slot])
                    bu, bsl, bw = bf_sl.next()
                    c1 = kb.op("vector", lambda e, slot=slot, bsl=bsl: e.tensor_copy(
                        out=mwb[bsl][:, 0:8, :], in_=stage[slot][:, 0:8, :]), waits=[tok] + bw, inc=ev)
                    c2 = kb.op("scalar", lambda e, slot=slot, bsl=bsl: e.activation(
                        out=mwb[bsl][:, 8:16, :], in_=stage[slot][:, 8:16, :], func=AF.Copy), waits=[tok] + bw, inc=c_rdy)
                    st_sl.release(u, c1, c2)
                    for f in range(4):
                        col = s_ * 4 + f
                        for k in range(DC):
                            last = (k == DC - 1)
                            waits = [c1, c2, cbtok] if (f == 0 and k == 0) else []
                            t2 = kb.op("tensor", mm(pst[:, col:col + 1], mwb[bsl][:, k, f * 128:(f + 1) * 128],
                                                    condb[:, k:k + 1], start=(k == 0), stop=last),
                                       waits=waits, inc=pe if (last and f == 3) else None)
                    bf_sl.release(bu, t2)
                self.mod_finish(i, pst, [t2, ldtok], ev)
            kb.flush()

    def mod_finish(self, i, pst, waits, ev):
        kb = self.kb
        misc, modp, tmp1, P_ = self.misc, self.modp, self.tmp1, self.par
        t3 = kb.op("vector", lambda e: e.tensor_tensor(out=modp[:, i, :], in0=pst, in1=misc[:, i, 0:96], op=ALU.add),
                   waits=waits, inc=ev)
        kb.op("vector", lambda e: e.tensor_scalar(out=tmp1[:, i, 0, :], in0=modp[:, i, 16:32], scalar1=1.0, scalar2=None,
                                                 op0=ALU.add), waits=[t3], inc=ev)
        tb = kb.op("vector", lambda e: e.tensor_scalar(out=tmp1[:, i, 1, :], in0=modp[:, i, 64:80], scalar1=1.0, scalar2=None,
                                                      op0=ALU.add), waits=[t3], inc=ev)
        kb.op("vector", lambda e: e.tensor_tensor(out=P_[:, i, 0, :], in0=tmp1[:, i, 0, :], in1=misc[:, i, 96:96 + DC],
                                                 op=ALU.mult), waits=[tb], inc=ev)
        kb.op("vector", lambda e: e.tensor_tensor(out=P_[:, i, 3, :], in0=tmp1[:, i, 1, :],
                                                 in1=misc[:, i, 96 + DC:96 + 2 * DC], op=ALU.mult), waits=[tb], inc=ev)
        for dst, src in ((1, 0), (2, 32), (4, 48), (5, 80)):
            kb.op("vector", lambda e, dst=dst, src=src: e.tensor_copy(out=P_[:, i, dst, :], in_=modp[:, i, src:src + DC]),
                  waits=[t3], inc=ev)

    def mod_side(self, i, name):
        kb = self.kb

        def setup(sb):
            stage = [sb(f"{name}ms{j}", [128, DC, 256], F32) for j in range(2)]
            mwb = [sb(f"{name}mw{j}", [128, DC, 256], BF16) for j in range(2)]
            st_rdy = [kb.sem(f"{name}msr{j}") for j in range(2)]
            sd, sa, spe = kb.sem(f"{name}msd"), kb.sem(f"{name}msa"), kb.sem(f"{name}mpe")
            st_sl, bf_sl = Slots(2), Slots(2)
            pst = self.psum[:, 6, 0:96]
            state = dict(j=0, last=None)
            NS = 6 * D // 256

            def emit_mm():
                p = state.pop("pend", None)
                if p is None:
                    return
                j, bu, bsl, c1, c2 = p
                for f in range(2):
                    col = j * 2 + f
                    for k in range(DC):
                        t2 = kb.op("tensor", mm(pst[:, col:col + 1], mwb[bsl][:, k, f * 128:(f + 1) * 128],
                                                self.condb[:, k:k + 1], start=(k == 0), stop=(k == DC - 1)),
                                   waits=[c1, c2] if (f == 0 and k == 0) else (), inc=spe if (f == 1 and k == DC - 1) else None)
                bf_sl.release(bu, t2)
                state["last"] = t2

            def step():
                emit_mm()
                j = state["j"]
                if j >= NS:
                    return
                state["j"] += 1
                u, slot, w = st_sl.next()
                for h in range(2):
                    tok = kb.dma("sync", stage[slot][:, h * 8:(h + 1) * 8, :],
                                 self.w_mod[i, h * 1024:(h + 1) * 1024, j * 256:(j + 1) * 256]
                                 .rearrange("(c p) n -> p c n", p=128), waits=w if h == 0 else (), inc=st_rdy[slot])
                bu, bsl, bw = bf_sl.next()
                c1 = kb.op("vector", lambda e: e.tensor_copy(out=mwb[bsl][:, 0:8, :], in_=stage[slot][:, 0:8, :]),
                           waits=[tok] + bw, inc=sd)
                c2 = kb.op("scalar", lambda e: e.activation(out=mwb[bsl][:, 8:16, :], in_=stage[slot][:, 8:16, :], func=AF.Copy),
                           waits=[tok] + bw, inc=sa)
                st_sl.release(u, c1, c2)
                state["pend"] = (j, bu, bsl, c1, c2)

            def finish():
                while state["j"] < NS:
                    step()
                emit_mm()
                self.mod_finish(i, pst, [state["last"]], sd)

            return step, finish

        return setup

    def phase_norm(self, xsrc, layer, which, name):
        kb, nc = self.kb, self.nc
        A = self.par[:, layer, 0 if which == 0 else 3, :]
        B = self.par[:, layer, 1 if which == 0 else 4, :]
        TG = 512
        NG = S // TG
        with ExitStack() as ps:
            sb = lambda n, sh, dt: ps.enter_context(nc.sbuf_tensor(n, list(sh), dt))
            xg = [sb(f"{name}x{i}", [128, DC, TG], F32) for i in range(3)]
            sq = [sb(f"{name}q{i}", [128, DC, TG], BF16) for i in range(2)]
            ln = [sb(f"{name}l{i}", [128, TG], F32) for i in range(2)]
            rs = [sb(f"{name}s{i}", [128, TG], F32) for i in range(2)]
            tm = [sb(f"{name}t{i}", [128, TG], F32) for i in range(4)]
            hb = [sb(f"{name}h{i}", [128, DC, TG], BF16) for i in range(2)]
            x_rdy = [kb.sem(f"{name}xr{i}") for i in range(3)]
            sa = kb.sem(f"{name}act")
            sp = kb.sem(f"{name}pe")
            sv = kb.sem(f"{name}dve")
            st = [kb.sem(f"{name}st{i}") for i in range(2)]
            xs, qs, pss, lns, rss, tms, hbs = Slots(3), Slots(2), Slots(2), Slots(2), Slots(2), Slots(4), Slots(2)
            xsrc_r = xsrc.rearrange("(c p) t -> p c t", p=128)
            hT_r = self.hT.rearrange("(c p) t -> p c t", p=128)
            G = {}

            L = {}

            def stage0(g):
                t0 = g * TG
                xu, xb, w = xs.next()
                for h in range(2):
                    xtok = kb.dma("sync", xg[xb][:, h * 8:(h + 1) * 8, :],
                                  xsrc_r[:, h * 8:(h + 1) * 8, t0:t0 + TG], waits=w if h == 0 else (), inc=x_rdy[xb])
                L[g] = (xu, xb, xtok)

            def stage1(g):
                t0 = g * TG
                xu, xb, xtok = L.pop(g)
                qu, qb, w = qs.next()
                sqtok = kb.op("scalar", lambda e, qb=qb, xb=xb: e.activation(out=sq[qb][:], in_=xg[xb][:], func=AF.Square),
                              waits=[xtok] + w, inc=sa)
                pu, pb, w = pss.next()
                pst = self.psum[:, pb, :]
                for c in range(DC):
                    petok = kb.op("tensor", mm(pst, self.ones[:], sq[qb][:, c, :], start=(c == 0), stop=(c == DC - 1)),
                                  waits=([sqtok] + w) if c == 0 else (), inc=sp if c == DC - 1 else None)
                qs.release(qu, petok)
                lu, lb, w = lns.next()
                lntok = kb.op("scalar", lambda e, lb=lb, pst=pst: e.activation(
                    out=ln[lb][:], in_=pst, func=AF.Ln, scale=1.0 / D, bias=self.epsc[:, 0:1]),
                    waits=[petok] + w, inc=sa)
                pss.release(pu, lntok)
                ru, rb, w = rss.next()
                rstok = kb.op("scalar", lambda e, lb=lb, rb=rb: e.activation(
                    out=rs[rb][:], in_=ln[lb][:], func=AF.Exp, scale=-0.5),
                    waits=[lntok] + w, inc=sa)
                lns.release(lu, rstok)
                G[g] = dict(xu=xu, xb=xb, xtok=xtok, sqtok=sqtok, ru=ru, rb=rb, rstok=rstok)

            def stage2(g):
                t0 = g * TG
                I = G.pop(g)
                xu, xb, xtok, sqtok, ru, rb, rstok = I["xu"], I["xb"], I["xtok"], I["sqtok"], I["ru"], I["rb"], I["rstok"]
                hu, hbk, hw = hbs.next()
                for c in range(DC):
                    tu, tbk, w = tms.next()
                    w2 = list(w)
                    if c == 0:
                        w2 += [rstok, xtok]
                    ttok = kb.op("vector", lambda e, xb=xb, rb=rb, c=c, tbk=tbk: e.scalar_tensor_tensor(
                        out=tm[tbk][:], in0=xg[xb][:, c, :], scalar=A[:, c:c + 1], in1=rs[rb][:],
                        op0=ALU.mult, op1=ALU.mult), waits=w2, inc=sv)
                    htok = kb.op("scalar", lambda e, hbk=hbk, c=c, tbk=tbk: e.activation(
                        out=hb[hbk][:, c, :], in_=tm[tbk][:], func=AF.Identity, bias=B[:, c:c + 1], scale=1.0),
                        waits=[ttok] + (hw if c == 0 else []), inc=sa)
                    tms.release(tu, htok)
                xs.release(xu, ttok, sqtok)
                rss.release(ru, ttok)
                sttok = kb.dma("scalar", hT_r[:, :, t0:t0 + TG], hb[hbk][:], waits=[htok], inc=st[hbk])
                hbs.release(hu, sttok)

            stage0(0)
            stage0(1)
            stage1(0)
            for g in range(NG):
                if g + 2 < NG:
                    stage0(g + 2)
                if g + 1 < NG:
                    stage1(g + 1)
                stage2(g)
            kb.op("scalar", lambda e: e.nop(), waits=[(s_, s_.n) for s_ in st])
            kb.flush()

    def gemm(self, name, AT, K, Wd, strips, TB, items_fn, epi_setup, pk=4, nbank=6,
             cast_engs=("gpsimd",), perm=False, side=None, side_every=7):
        kb, nc = self.kb, self.nc
        KC = K // 128
        CS = sum(w for _, w in strips[0])
        NTB = S // TB
        NPIECE = KC // pk
        assert KC % pk == 0
        with ExitStack() as ps:
            sb = lambda n, sh, dt: ps.enter_context(nc.sbuf_tensor(n, list(sh), dt))
            abuf = sb(f"{name}A", [128, KC, TB], BF16)
            pbuf = sb(f"{name}P", [128, KC, TB], BF16) if perm else None
            NST = 3
            stage = [sb(f"{name}S{i}", [128, pk, CS], F32) for i in range(NST)]
            wbf = [sb(f"{name}W{i}", [128, KC, CS], BF16) for i in range(2)]
            epi = epi_setup(ps, sb)
            side_step, side_finish = side(sb) if side is not None else (None, None)
            a_rdy = kb.sem(f"{name}ar")
            st_rdy = [kb.sem(f"{name}sr{i}") for i in range(NST)]
            w_rdy = {(e_, i): kb.sem(f"{name}wr{e_[:1]}{i}") for e_ in ("vector", "scalar", "gpsimd") for i in range(2)}
            pe_done = kb.sem(f"{name}pe")
            pm = kb.sem(f"{name}pm") if perm else None
            npc = [0]
            st_sl = Slots(NST)
            AT_r = AT.rearrange("(c p) t -> p c t", p=128)
            seq = [(tb, s_) for tb in range(NTB) for s_ in range(len(strips))]
            strip_pe_tok = {}
            a_tok = {}
            w_tok = {}
            bank_sl = {}
            nitem = [0]

            def load_A_tile(tb, tt, waits):
                if tb >= NTB or (tb, tt) in a_tok:
                    return
                a_tok[(tb, tt)] = kb.dma("sync", abuf[:, :, tt * 512:(tt + 1) * 512],
                                         AT_r[:, :, tb * TB + tt * 512:tb * TB + (tt + 1) * 512],
                                         waits=waits, inc=a_rdy)

            def load_A(tb):
                if tb >= NTB:
                    return
                waits = []
                if tb >= 1:
                    waits = [strip_pe_tok[tb * len(strips) - 1]]
                for tt in range(TB // 512):
                    load_A_tile(tb, tt, waits)

            def load_W(q):
                if q >= len(seq) or q in w_tok:
                    return
                tb, s_ = seq[q]
                ws = q % 2
                lastc = {}
                for pc in range(NPIECE):
                    u, slot, w = st_sl.next()
                    c0 = 0
                    for j, (col0, width) in enumerate(strips[s_]):
                        tok = kb.dma("sync", stage[slot][:, :, c0:c0 + width],
                                     Wd[pc * pk * 128:(pc + 1) * pk * 128, col0:col0 + width]
                                     .rearrange("(c p) n -> p c n", p=128),
                                     waits=w if j == 0 else (), inc=st_rdy[slot])
                        c0 += width
                    waits = [tok]
                    if q >= 2:
                        waits.append(strip_pe_tok[q - 2])
                    ce = cast_engs[npc[0] % len(cast_engs)]
                    if q == 0:
                        ce = ("vector", "scalar", "gpsimd")[pc % 3]
                    npc[0] += 1
                    if ce == "scalar":
                        cfn = lambda e, slot=slot, ws=ws, pc=pc: e.activation(
                            out=wbf[ws][:, pc * pk:(pc + 1) * pk, :], in_=stage[slot][:], func=AF.Copy)
                    else:
                        cfn = lambda e, slot=slot, ws=ws, pc=pc: e.tensor_copy(
                            out=wbf[ws][:, pc * pk:(pc + 1) * pk, :], in_=stage[slot][:])
                    ctok = kb.op(ce, cfn, waits=waits, inc=w_rdy[(ce, ws)])
                    st_sl.release(u, ctok)
                    lastc[ce] = ctok
                w_tok[q] = list(lastc.values())

            perm_tok = {}

            def perm_ops(tb, s_):
                ns = len(strips) // 3
                if s_ < 8:
                    src, dst, k0, key = abuf, pbuf, 2 * s_, (tb, 1)
                    w0 = [a_tok[(tb, TB // 512 - 1)]] + ([strip_pe_tok[(tb - 1) * len(strips) + 2 * ns - 1]] if tb > 0 else [])
                elif ns <= s_ < ns + 8:
                    src, dst, k0, key = pbuf, abuf, 2 * (s_ - ns), (tb, 2)
                    w0 = [perm_tok[(tb, 1)], strip_pe_tok[tb * len(strips) + ns - 1]]
                else:
                    return
                for k in (k0, k0 + 1):
                    perm_tok[key] = kb.op("scalar", lambda e, src=src, dst=dst, k=k: e.activation(
                        out=dst[:, k, :].rearrange("p (r a) -> p r a", r=4),
                        in_=src[:, k, :].rearrange("p (a r) -> p r a", r=4), func=AF.Copy), waits=w0, inc=pm)

            def perm_wait(tb, s_):
                ns = len(strips) // 3
                if s_ == ns:
                    return [perm_tok[(tb, 1)]]
                if s_ == 2 * ns:
                    return [perm_tok[(tb, 2)]]
                return []

            load_W(0)
            load_A(0)
            for q, (tb, s_) in enumerate(seq):
                cur_abuf = abuf
                if perm:
                    ns_ = len(strips) // 3
                    cur_abuf = pbuf if ns_ <= s_ < 2 * ns_ else abuf
                first = True
                last_tok = None
                its = list(items_fn(tb, s_))
                if all("tt" in it for it in its):
                    its.sort(key=lambda it: it["tt"])
                last_of_tt = {}
                if s_ == len(strips) - 1 and all("tt" in it for it in its):
                    for ii, it in enumerate(its):
                        last_of_tt[it["tt"]] = ii
                for ii, it in enumerate(its):
                    if ii == (0 if len(its) <= 8 else 3):
                        load_W(q + 1)
                        if perm:
                            perm_ops(tb, s_)
                    nbk = it["nb"]
                    sl = bank_sl.setdefault(nbk, Slots(nbank // nbk))
                    u, slot, w = sl.next()
                    banks = [self.psum[:, slot * nbk + j, :] for j in range(nbk)]
                    waits = list(w)
                    if s_ == 0 and "tt" in it:
                        waits.append(a_tok[(tb, it["tt"])])
                    if first and s_ == 0 and "tt" in it:
                        waits += w_tok[q]
                        if perm:
                            waits += perm_wait(tb, s_)
                        first = False
                    if first:
                        waits += w_tok[q] + [a_tok[(tb, TB // 512 - 1)]]
                        if perm:
                            waits += perm_wait(tb, s_)
                        first = False
                    nmm = len(it["mms"])
                    for mi, (bj, wc0, wcs, tokap, ncols_out) in enumerate(it["mms"]):
                        for k in range(KC):
                            wap = wbf[q % 2][:, k, wc0:wc0 + wcs]
                            aap = tokap(cur_abuf, k)
                            last = (mi == nmm - 1 and k == KC - 1)
                            if it["kind"] == "fm":
                                f = mm(banks[bj][:, 0:ncols_out], wap, aap, start=(k == 0), stop=(k == KC - 1))
                            else:
                                f = mm(banks[bj][:, 0:ncols_out], aap, wap, start=(k == 0), stop=(k == KC - 1))
                            tok = kb.op("tensor", f, waits=waits if (mi == 0 and k == 0) else (),
                                        inc=pe_done if last else None)
                    last_tok = tok
                    nitem[0] += 1
                    rel = epi(it, banks, it["meta"], tok)
                    sl.release(u, *rel)
                    if side_step is not None and nitem[0] % side_every == 0:
                        side_step()
                    if last_of_tt and last_of_tt.get(it["tt"]) == ii:
                        load_A_tile(tb + 1, it["tt"], [tok])
                strip_pe_tok[q] = last_tok
                if s_ == len(strips) - 1:
                    load_A(tb + 1)
            if side_finish is not None:
                side_finish()
            epi(None, None, None, None)
            kb.flush()

    def resid_epi(self, name, xsrc, xdst, gcol, TB, norm=None, upi=2):
        kb = self.kb
        NTT = TB // 512

        def setup(ps, sb):
            NX = 4
            xt = [sb(f"{name}x{i}", [128, 512], F32) for i in range(NX)]
            x_rdy = [kb.sem(f"{name}xr{i}") for i in range(NX)]
            st_done = [kb.sem(f"{name}sd{i}") for i in range(NX)]
            dv = kb.sem(f"{name}dv")
            xsl = Slots(NX)
            if norm is not None:
                layer, which = norm
                A = self.par[:, layer, 0 if which == 0 else 3, :]
                B = self.par[:, layer, 1 if which == 0 else 4, :]
                sqb = [sb(f"{name}sq{i}", [128, 512], BF16) for i in range(3)]
                rsb = sb(f"{name}rs", [128, NTT, 512], F32)
                lnb = sb(f"{name}ln", [128, 512], F32)
                xq = sb(f"{name}xq", [128, 8, 512], F32)
                tmb = [sb(f"{name}tm{i}", [128, 512], F32) for i in range(3)]
                hb = sb(f"{name}hb", [128, DC, 512], BF16)
                sa = kb.sem(f"{name}sa")
                spn = kb.sem(f"{name}spn")
                xq_rdy = kb.sem(f"{name}xq")
                hst = kb.sem(f"{name}hst")
                sqsl, tmsl = Slots(3), Slots(3)
                xdst_r = xdst.rearrange("(c p) t -> p c t", p=128)
                hT_r = self.hT.rearrange("(c p) t -> p c t", p=128)
            pend = []
            st = dict(cur_tb=None, nstat={}, stat_last={}, bank_free={}, todo=[], store_tok={},
                      xq_free=[], hb_free=[], rs_free={}, ln_free=[])

            def emit_stat(p):
                tb, tt, sqtok, qu, qs = p
                n = st["nstat"].get((tb, tt), 0)
                w = [sqtok]
                if n == 0:
                    w += st["bank_free"].get(tt, [])
                tok = kb.op("tensor", mm(self.psum[:, 6 + tt, :], self.ones[:], sqb[qs][:],
                                         start=(n == 0), stop=(n == DC - 1)), waits=w, inc=spn)
                sqsl.release(qu, tok)
                st["nstat"][(tb, tt)] = n + 1
                st["stat_last"][(tb, tt)] = tok

            def plan_norm(tb):
                while st["todo"]:
                    run_unit()
                for tt in range(NTT):
                    l1 = kb.op("scalar", lambda e, tt=tt: e.activation(
                        out=lnb[:], in_=self.psum[:, 6 + tt, :], func=AF.Ln, scale=1.0 / D, bias=self.epsc[:, 0:1]),
                        waits=[st["stat_last"][(tb, tt)]] + st["ln_free"], inc=sa)
                    st["bank_free"][tt] = [l1]
                    l2 = kb.op("scalar", lambda e, tt=tt: e.activation(out=rsb[:, tt, :], in_=lnb[:], func=AF.Exp, scale=-0.5),
                               waits=[l1] + st["rs_free"].get(tt, []), inc=sa)
                    st["ln_free"] = [l2]
                    st["rs_tok", tt] = l2
                for tt in range(NTT):
                    for half in range(2):
                        st["todo"].append(("ld", tb, tt, half))
                        for c in range(8):
                            st["todo"].append(("ch", tb, tt, half, c))
                    st["todo"].append(("st", tb, tt))

            def run_unit():
                if not st["todo"]:
                    return
                u = st["todo"].pop(0)
                if False:
                    pass
                elif u[0] == "ld":
                    _, tb, tt, half = u
                    t0 = tb * TB + tt * 512
                    w = [st["store_tok"][(tb, tt, c)] for c in range(half * 8, half * 8 + 8)] + st["xq_free"]
                    st["xq_tok"] = kb.dma("sync", xq[:], xdst_r[:, half * 8:half * 8 + 8, t0:t0 + 512], waits=w, inc=xq_rdy)
                elif u[0] == "ch":
                    _, tb, tt, half, c = u
                    cc = half * 8 + c
                    tu, ts_, w = tmsl.next()
                    w2 = list(w)
                    if c == 0:
                        w2 += [st["xq_tok"], st["rs_tok", tt]]
                    d1 = kb.op("vector", lambda e, c=c, cc=cc, tt=tt, ts_=ts_: e.scalar_tensor_tensor(
                        out=tmb[ts_][:], in0=xq[:, c, :], scalar=A[:, cc:cc + 1], in1=rsb[:, tt, :],
                        op0=ALU.mult, op1=ALU.mult), waits=w2, inc=dv)
                    a1 = kb.op("scalar", lambda e, cc=cc, ts_=ts_: e.activation(
                        out=hb[:, cc, :], in_=tmb[ts_][:], func=AF.Identity, bias=B[:, cc:cc + 1], scale=1.0),
                        waits=[d1] + (st["hb_free"] if cc == 0 else []), inc=sa)
                    tmsl.release(tu, a1)
                    if c == 7:
                        st["xq_free"] = [d1]
                    st["rs_free"][tt] = [d1]
                    st["h_tok"] = a1
                else:
                    _, tb, tt = u
                    t0 = tb * TB + tt * 512
                    st["hb_free"] = [kb.dma("scalar", hT_r[:, :, t0:t0 + 512], hb[:], waits=[st["h_tok"]], inc=hst)]

            def epi(it, banks, meta, pe_tok):
                if it is None:
                    if norm is not None:
                        while pend:
                            emit_stat(pend.pop(0))
                        plan_norm(st["cur_tb"])
                        while st["todo"]:
                            run_unit()
                        kb.op("scalar", lambda e: e.nop(), waits=[(hst, hst.n)])
                    kb.op("scalar", lambda e: e.nop(), waits=[(s_, s_.n) for s_ in st_done])
                    return
                fch, t0 = meta
                tb, tt = t0 // TB, (t0 % TB) // 512
                if norm is not None and st["cur_tb"] is not None and tb != st["cur_tb"]:
                    while pend:
                        emit_stat(pend.pop(0))
                    plan_norm(st["cur_tb"])
                st["cur_tb"] = tb
                u, b, w = xsl.next()
                xtok = kb.dma("sync", xt[b][:], xsrc[fch * 128:(fch + 1) * 128, t0:t0 + 512], waits=w, inc=x_rdy[b])
                dtok = kb.op("vector", lambda e, b=b, fch=fch, bank=banks[0]: e.scalar_tensor_tensor(
                    out=xt[b][:], in0=bank, scalar=gcol[:, fch:fch + 1], in1=xt[b][:],
                    op0=ALU.mult, op1=ALU.add), waits=[pe_tok, xtok], inc=dv)
                stok = kb.dma("scalar", xdst[fch * 128:(fch + 1) * 128, t0:t0 + 512], xt[b][:], waits=[dtok], inc=st_done[b])
                if norm is None:
                    xsl.release(u, stok)
                    return [dtok]
                st["store_tok"][(tb, tt, fch)] = stok
                qu, qs, w = sqsl.next()
                sqtok = kb.op("scalar", lambda e, b=b, qs=qs: e.activation(out=sqb[qs][:], in_=xt[b][:], func=AF.Square),
                              waits=[dtok] + w, inc=sa)
                xsl.release(u, stok, sqtok)
                pend.append((tb, tt, sqtok, qu, qs))
                if len(pend) > 2:
                    emit_stat(pend.pop(0))
                for _ in range(upi):
                    run_unit()
                return [dtok]

            return epi

        return setup

    def resid_items(self, TB, ncols_strip):
        def items(tb, s_):
            for f in range(ncols_strip // 128):
                for tt in range(TB // 512):
                    yield dict(kind="fm", nb=1, tt=tt,
                               mms=[(0, f * 128, 128, (lambda a, k, tt=tt: a[:, k, tt * 512:(tt + 1) * 512]), 512)],
                               meta=(s_ * (ncols_strip // 128) + f, tb * TB + tt * 512))
        return items

    def phase_resid_gemm(self, name, AT, K, Wd, xsrc, xdst, gcol, TB, CS, norm=None, upi=2):
        strips = [[(c0, CS)] for c0 in range(0, D, CS)]
        self.gemm(name, AT, K, Wd, strips, TB, self.resid_items(TB, CS), self.resid_epi(name, xsrc, xdst, gcol, TB, norm, upi),
                  cast_engs=("vector", "gpsimd", "vector"))

    def phase_ffn_up(self, name, Wd, side=None):
        kb = self.kb
        TB = 2048
        strips = [[(256 * j, 256), (DFF + 256 * j, 256)] for j in range(DFF // 256)]
        hid = self.hid

        def items(tb, s_):
            for jj in range(2):
                for tt in range(TB // 512):
                    tk = (lambda a, k, tt=tt: a[:, k, tt * 512:(tt + 1) * 512])
                    yield dict(kind="fm", nb=2, tt=tt,
                               mms=[(0, jj * 128, 128, tk, 512), (1, 256 + jj * 128, 128, tk, 512)],
                               meta=(2 * s_ + jj, tb * TB + tt * 512))

        def setup(ps, sb):
            NB = 3
            sg = [sb(f"{name}g{i}", [128, 512], F32) for i in range(NB)]
            ob = [sb(f"{name}o{i}", [128, 512], BF16) for i in range(NB)]
            st_done = [kb.sem(f"{name}sd{i}") for i in range(NB)]
            sa = kb.sem(f"{name}sa")
            dv = kb.sem(f"{name}dv")
            gsl, osl = Slots(NB), Slots(NB)

            def epi(it, banks, meta, pe_tok):
                if it is None:
                    kb.op("scalar", lambda e: e.nop(), waits=[(s_, s_.n) for s_ in st_done])
                    return
                ch, t0 = meta
                gu, gb, w = gsl.next()
                atok = kb.op("scalar", lambda e, gb=gb, bank=banks[0]: e.activation(out=sg[gb][:], in_=bank, func=AF.Silu),
                             waits=[pe_tok] + w, inc=sa)
                ou, obk, w = osl.next()
                dtok = kb.op("vector", lambda e, gb=gb, obk=obk, bank=banks[1]: e.tensor_tensor(
                    out=ob[obk][:], in0=bank, in1=sg[gb][:], op=ALU.mult), waits=[atok] + w, inc=dv)
                gsl.release(gu, dtok)
                stok = kb.dma("scalar", hid[ch * 128:(ch + 1) * 128, t0:t0 + 512], ob[obk][:], waits=[dtok], inc=st_done[obk])
                osl.release(ou, stok)
                return [dtok]

            return epi

        self.gemm(name, self.hT, D, Wd, strips, TB, items, setup, side=side)

    def phase_attn_bias(self):
        kb, nc = self.kb, self.nc
        with ExitStack() as ps:
            sb = lambda n, sh, dt: ps.enter_context(nc.sbuf_tensor(n, list(sh), dt))
            rb = sb("ab_rb", [33, 48], F32)
            oh = sb("ab_oh", [33, 3, 384], F32)
            vec = sb("ab_vec", [16, 3, 384], F32)
            ld = kb.sem("ab_ld")
            pe = kb.sem("ab_pe")
            dv = kb.sem("ab_dv")
            st = kb.sem("ab_st")
            mtok = kb.op("gpsimd", lambda e: e.memset(rb[32:33, :], NEG), inc=dv)
            kb.dma("sync", rb[0:32, :], self.relb[:, :], inc=ld)
            ltok = kb.dma("sync", oh[:], self.c_onehot[:, :, :], inc=ld)
            for g in range(3):
                pst = self.psum[0:16, g, 0:384]
                ptok = kb.op("tensor", mm(pst, rb[:, g * 16:(g + 1) * 16], oh[:, g, :]), waits=[ltok, mtok], inc=pe)
                ctok = kb.op("vector", lambda e, g=g, pst=pst: e.tensor_copy(out=vec[:, g, :], in_=pst), waits=[ptok], inc=dv)
                kb.dma("scalar", self.vecd[g], vec[:, g, :], waits=[ctok], inc=st)
            kb.op("scalar", lambda e: e.nop(), waits=[(st, st.n)])
            kb.flush()

    @staticmethod
    def grp_tok512(dil, tt):
        if dil == 1:
            return lambda a, k: a[:, k, tt * 512:(tt + 1) * 512]
        if dil == 4:
            return lambda a, k: a[:, k, tt:2048:4]
        return lambda a, k: a[:, k, :].rearrange("p (a r) -> p r a", r=16)[:, 4 * tt:4 * tt + 4, :]

    @staticmethod
    def grp_tok128(dil, blk):
        if dil == 1:
            return lambda a, k: a[:, k, blk * 128:(blk + 1) * 128]
        if dil == 4:
            r, ab = blk // 4, blk % 4
            return lambda a, k: a[:, k, r + 512 * ab:r + 512 * ab + 509:4]
        return lambda a, k: a[:, k, blk:2048:16]

    def phase_qkv(self):
        kb = self.kb
        name = "qkv"
        TB = 2048
        strips = [[(c0, 512)] for c0 in range(0, 9 * D, 512)]

        def items(tb, s_):
            g, t, hq = s_ // 12, (s_ % 12) // 4, s_ % 4
            dil = DILS[g]
            if t < 2:
                for f in range(4):
                    for tt in range(4):
                        yield dict(kind="fm", nb=1, tt=tt, mms=[(0, f * 128, 128, self.grp_tok512(1, tt), 512)],
                                   meta=("qk", g, t, 4 * hq + f, tb * TB + tt * 512))
            else:
                for blk in range(16):
                    yield dict(kind="tm", nb=1, mms=[(0, 0, 512, self.grp_tok128(1, blk), 512)],
                               meta=("v", g, hq, tb * 16 + blk))

        def setup(ps, sb):
            sqb = [sb(f"{name}q{i}", [128, 512], BF16) for i in range(3)]
            lnb = [sb(f"{name}l{i}", [128, 512], F32) for i in range(2)]
            rsb = [sb(f"{name}r{i}", [128, 512], F32) for i in range(2)]
            NO = 4
            ob = [sb(f"{name}o{i}", [128, 512], BF16) for i in range(NO)]
            st_done = [kb.sem(f"{name}sd{i}") for i in range(NO)]
            sa = kb.sem(f"{name}sa")
            dv = kb.sem(f"{name}dv")
            sp = kb.sem(f"{name}sp")
            qsl, lsl, rsl, osl, ssl = Slots(3), Slots(2), Slots(2), Slots(NO), Slots(2)
            pend = []
            cnt = [dv.n]

            def flush_pending():
                if not pend:
                    return
                (bank, meta, qu, qb, sqtok) = pend.pop()
                _, g, t, head, p0 = meta
                su, sbk, w = ssl.next()
                pst = self.psum[:, 6 + sbk, :]
                ptok = kb.op("tensor", mm(pst, self.ones[:], sqb[qb][:]), waits=[sqtok] + w, inc=sp)
                qsl.release(qu, ptok)
                lu, lb, w = lsl.next()
                ltok = kb.op("scalar", lambda e, lb=lb, pst=pst: e.activation(
                    out=lnb[lb][:], in_=pst, func=AF.Ln, scale=1.0 / 128, bias=self.epsc[:, 0:1]),
                    waits=[ptok] + w, inc=sa)
                ssl.release(su, ltok)
                ru, rb, w = rsl.next()
                rtok = kb.op("scalar", lambda e, lb=lb, rb=rb: e.activation(
                    out=rsb[rb][:], in_=lnb[lb][:], func=AF.Exp, scale=-0.5), waits=[ltok] + w, inc=sa)
                lsl.release(lu, rtok)
                ou, obk, w = osl.next()
                dtok = kb.op("vector", lambda e, bank=bank, rb=rb, obk=obk, t=t, g=g: e.scalar_tensor_tensor(
                    out=ob[obk][:], in0=bank, scalar=self.gq[:, t, g:g + 1], in1=rsb[rb][:],
                    op0=ALU.mult, op1=ALU.mult), waits=[rtok] + w, inc=dv)
                rsl.release(ru, dtok)
                dst = (self.qTd if t == 0 else self.kTd)[g, head, :, p0:p0 + 512]
                stok = kb.dma("scalar", dst, ob[obk][:], waits=[dtok], inc=st_done[obk])
                osl.release(ou, stok)

            def epi(it, banks, meta, pe_tok):
                if it is None:
                    flush_pending()
                    kb.op("scalar", lambda e: e.nop(), waits=[(s_, s_.n) for s_ in st_done])
                    return
                cnt[0] += 1
                mytok = (dv, cnt[0])
                if meta[0] == "qk":
                    qu, qb, w = qsl.next()
                    sqtok = kb.op("scalar", lambda e, qb=qb, bank=banks[0]: e.activation(
                        out=sqb[qb][:], in_=bank, func=AF.Square), waits=[pe_tok] + w, inc=sa)
                    flush_pending()
                    pend.append((banks[0], meta, qu, qb, sqtok))
                    return [LateTok(mytok)]
                flush_pending()
                _, g, hq, B = meta
                ou, obk, w = osl.next()
                dtok = kb.op("vector", lambda e, obk=obk, bank=banks[0]: e.tensor_copy(out=ob[obk][:], in_=bank),
                             waits=[pe_tok] + w, inc=dv)
                assert dtok[1] == mytok[1]
                dst = self.vd[g, 4 * hq:4 * hq + 4, :, B, :].rearrange("h p d -> p h d")
                stok = kb.dma("scalar", dst, ob[obk][:].rearrange("p (h d) -> p h d", h=4), waits=[dtok], inc=st_done[obk])
                osl.release(ou, stok)
                return [dtok]

            return epi

        self.gemm(name, self.hT, D, self.w_qkv, strips, TB, items, setup, perm=True,
                  cast_engs=("gpsimd", "vector", "gpsimd"))


    def phase_attn(self):
        kb, nc = self.kb, self.nc
        name = "at"
        with ExitStack() as ps:
            sb = lambda n, sh, dt: ps.enter_context(nc.sbuf_tensor(n, list(sh), dt))
            qb = [sb(f"{name}q{i}", [128, S], BF16) for i in range(2)]
            kbf = [sb(f"{name}k{i}", [128, S], BF16) for i in range(2)]
            vb = [sb(f"{name}v{i}", [128, 32, 128], BF16) for i in range(2)]
            bm = [sb(f"{name}b{i}", [128, 256], F32) for i in range(2)]
            bmr = [sb(f"{name}br{i}", [128, 256], F32) for i in range(2)]
            bmh = [sb(f"{name}bh{i}", [128, 2, 256], BF16) for i in range(2)]
            revm = sb(f"{name}rev", [128, 128], F32)
            ldb = [kb.sem(f"{name}ldb{i}") for i in range(2)]
            ldr = kb.sem(f"{name}ldr")
            rtok = kb.dma("sync", revm[:], self.c_rev[:, :], inc=ldr)
            bm_w = {}
            tb_ = [sb(f"{name}t{i}", [128, 256], F32) for i in range(3)]
            pb = [sb(f"{name}p{i}", [128, 256], BF16) for i in range(6)]
            acc = [sb(f"{name}a{i}", [128, 2, S], F32) for i in range(2)]
            ob = [sb(f"{name}o{i}", [128, S], BF16) for i in range(2)]
            lnd = sb(f"{name}ln", [128, S], F32)
            ld = [kb.sem(f"{name}ld{i}") for i in range(2)]
            sp = kb.sem(f"{name}sp")
            sv = kb.sem(f"{name}sv")
            sa = kb.sem(f"{name}sa")
            spv = kb.sem(f"{name}spv")
            st = [kb.sem(f"{name}st{i}") for i in range(2)]
            lsl, ssl, tsl, psl, odsl, asl, osl = Slots(2), Slots(4), Slots(3), Slots(6), Slots(3), Slots(2), Slots(2)
            units = [(h, g) for h in range(16) for g in range(3)]
            NB = 32
            st_ = {}

            def blocks(g):
                dil = DILS[g]
                nper = 16 // dil
                out = []
                for B in range(NB):
                    tbk, rem = B // 16, B % 16
                    r, ab = rem // nper, rem % nper
                    if ab > 0:
                        prev = B - 1
                    elif tbk == 1:
                        prev = r * nper + nper - 1
                    else:
                        prev = None
                    off = 2048 * tbk + 128 * ab * dil + r
                    out.append((prev, slice(off, off + 127 * dil + 1, dil)))
                return out

            pending_j = []

            def load_unit(ui):
                h, g = units[ui]
                u, slot, w = lsl.next()
                src = bass.AP(self.vecd.tensor, (g * 16 + h) * 384, [[1, 128], [1, 256]])
                htok = kb.dma("sync", bmr[slot][:], src, waits=w, inc=ldb[slot])
                kb.dma("sync", qb[slot][:], self.qTd[g, h], inc=ld[slot])
                kb.dma("sync", kbf[slot][:], self.kTd[g, h], inc=ld[slot])
                tok = kb.dma("sync", vb[slot][:], self.vd[g, h], inc=ld[slot])
                st_[ui] = dict(u=u, slot=slot, tok=tok, bmtok=None, blocks=blocks(g))

                def finish_j():
                    jt = kb.op("tensor", mm(self.psum[:, 7, 0:256], revm[:], bmr[slot][:]),
                               waits=[htok, rtok] + bm_w.get("w", []), inc=sp)
                    b0 = kb.op("scalar", lambda e: e.activation(out=bm[slot][:], in_=self.psum[:, 7, 0:256], func=AF.Copy),
                               waits=[jt], inc=sa)
                    bm_w["w"] = [b0]
                    b1 = kb.op("vector", lambda e: e.tensor_copy(out=bmh[slot][:, 0, :], in_=bm[slot][:]),
                               waits=[b0], inc=sv)
                    st_[ui]["bmtok"] = kb.op("vector", lambda e: e.tensor_tensor(
                        out=bmh[slot][:, 1, :], in0=bm[slot][:], in1=bmh[slot][:, 0, :], op=ALU.subtract), waits=[b1], inc=sv)

                pending_j.append(finish_j)

            seqn = [(ui, B) for ui in range(len(units)) for B in range(NB)]
            info = {}
            acc_state = {}
            lastpv = {}

            def stageA(n):
                ui, B = seqn[n]
                U = st_[ui]
                slot = U["slot"]
                prev, csl = U["blocks"][B]
                wd = 256 if prev is not None else 128
                su, sslot, w = ssl.next()
                S_ = self.psum[:, sslot, 0:256]
                bs = slice(B * 128, (B + 1) * 128)
                kb.op("tensor", mm(S_[:, 0:wd], self.ident[:], bmh[slot][:, 0, 0:wd], start=True, stop=False),
                      waits=w + ([U["tok"], U["bmtok"]] if B == 0 else []))
                kb.op("tensor", mm(S_[:, 0:wd], self.ident[:], bmh[slot][:, 1, 0:wd], start=False, stop=False))
                tok = kb.op("tensor", mm(S_[:, 0:128], kbf[slot][:, bs], qb[slot][:, bs], start=False, stop=(prev is None)),
                            inc=sp if prev is None else None)
                if prev is not None:
                    pbs = slice(prev * 128, (prev + 1) * 128)
                    tok = kb.op("tensor", mm(S_[:, 128:256], kbf[slot][:, pbs], qb[slot][:, bs], start=False, stop=True), inc=sp)
                pu, pslot, w = psl.next()
                ptok = kb.op("scalar", lambda e, S_=S_, pslot=pslot, wd=wd: e.activation(
                    out=pb[pslot][:, 0:wd], in_=S_[:, 0:wd], func=AF.Exp), waits=[tok] + w, inc=sa)
                ssl.release(su, ptok)
                info[n] = dict(pu=pu, pslot=pslot, ptok=ptok, prev=prev, csl=csl, wd=wd)

            def stageB(n):
                ui, B = seqn[n]
                h, g = units[ui]
                U = st_[ui]
                slot = U["slot"]
                I = info.pop(n)
                prev, csl, pslot = I["prev"], I["csl"], I["pslot"]
                ou, oslot, w = odsl.next()
                OD = self.psum[:, 4 + oslot, 0:256]
                P_ = pb[pslot]
                kb.op("tensor", mm(OD[:, 0:128], vb[slot][:, B, :], P_[:, 0:128], start=True, stop=(prev is None)),
                      waits=[I["ptok"]] + w)
                if prev is not None:
                    kb.op("tensor", mm(OD[:, 0:128], vb[slot][:, prev, :], P_[:, 128:256], start=False, stop=True))
                tok = kb.op("tensor", mm(OD[:, 128:256], self.ones[:], P_[:, 0:128], start=True, stop=(prev is None)),
                            inc=spv if prev is None else None)
                if prev is not None:
                    tok = kb.op("tensor", mm(OD[:, 128:256], self.ones[:], P_[:, 128:256], start=False, stop=True), inc=spv)
                psl.release(I["pu"], tok)
                lastpv[ui] = tok
                if g == 0 and B == 0:
                    au, aslot, aw = asl.next()
                    acc_state[h] = dict(au=au, aslot=aslot, aw=aw, toks={0: [], 1: [], 2: []})
                A = acc_state[h]
                dst = acc[A["aslot"]][:, :, csl]
                src = OD.rearrange("p (a n) -> p a n", a=2)
                if g == 0:
                    etok = kb.op("scalar", lambda e, dst=dst, src=src: e.activation(out=dst, in_=src, func=AF.Copy),
                                 waits=[tok] + (A["aw"] if B == 0 else []), inc=sa)
                else:
                    wprev = [A["toks"][g - 1][-1]] if B == 0 else []
                    etok = kb.op("vector", lambda e, dst=dst, src=src: e.tensor_tensor(
                        out=dst, in0=src, in1=dst, op=ALU.add), waits=[tok] + wprev, inc=sv)
                A["toks"][g].append(etok)
                odsl.release(ou, etok)
                if B == NB - 1:
                    lsl.release(U["u"], tok)
                    if g == 2:
                        finalize(h)

            fin_q = []

            def finalize(h):
                A = acc_state.pop(h)
                a = acc[A["aslot"]]
                ou, oslot, w = osl.next()
                last = A["toks"][2][-1]
                for ci in range(8):
                    cs = slice(ci * 512, (ci + 1) * 512)

                    def chunk(ci=ci, cs=cs):
                        t1 = kb.op("scalar", lambda e: e.activation(out=lnd[:, cs], in_=a[:, 1, cs], func=AF.Ln),
                                   waits=[last] + fin_w.get(ci, []), inc=sa)
                        t2 = kb.op("scalar", lambda e: e.activation(out=a[:, 1, cs], in_=lnd[:, cs], func=AF.Exp, scale=-1.0),
                                   waits=[t1], inc=sa)
                        fin_w[ci] = [t2]
                        t3 = kb.op("vector", lambda e: e.tensor_tensor(
                            out=ob[oslot][:, cs], in0=a[:, 0, cs], in1=a[:, 1, cs], op=ALU.mult),
                            waits=[t2] + (w if ci == 0 else []), inc=sv)
                        if ci == 7:
                            asl.release(A["au"], t3)
                            stok = kb.dma("scalar", self.oTd[h * 128:(h + 1) * 128, :], ob[oslot][:], waits=[t3], inc=st[oslot])
                            osl.release(ou, stok)

                    fin_q.append(chunk)

            fin_w = {}
            N = len(seqn)
            load_unit(0)
            load_unit(1)
            while pending_j:
                pending_j.pop(0)()
            LOOK = 3
            for n in range(-LOOK, N):
                m = n + LOOK
                if m < N:
                    stageA(m)
                if n >= 0:
                    stageB(n)
                    ui, B = seqn[n]
                    if B == NB - 1 and ui + 2 < len(units):
                        load_unit(ui + 2)
                    if B == 3:
                        while pending_j:
                            pending_j.pop(0)()
                    if fin_q:
                        fin_q.pop(0)()
            while fin_q:
                fin_q.pop(0)()
            kb.op("scalar", lambda e: e.nop(), waits=[(s_, s_.n) for s_ in st])
            kb.flush()


    def phase_ret_gemm(self):
        kb, nc = self.kb, self.nc
        name = "rg"
        TB = 2048
        strips = [[(c0, 512)] for c0 in range(0, 6 * D, 512)]
        cosb = self.gsb_phase = None

        def items(tb, s_):
            c0 = s_ * 512
            if c0 < 2 * D:
                t = 0 if c0 < D else 1
                hbase = ((c0 % D) // 256)
                for hl in range(2):
                    for tt in range(4):
                        tk = (lambda a, k, tt=tt: a[:, k, tt * 512:(tt + 1) * 512])
                        yield dict(kind="fm", nb=2, tt=tt,
                                   mms=[(0, hl * 256, 128, tk, 512), (1, hl * 256 + 128, 128, tk, 512)],
                                   meta=("rot", t, hbase + hl, tb * TB + tt * 512))
            elif c0 < 4 * D:
                head = (c0 - 2 * D) // 512
                for blk in range(16):
                    yield dict(kind="tm", nb=2,
                               mms=[(0, 0, 512, (lambda a, k, blk=blk: a[:, k, blk * 128:(blk + 1) * 128]), 512)],
                               meta=("v", head, tb * 16 + blk))
            else:
                fb = (c0 - 4 * D) // 128
                for f in range(4):
                    for tt in range(4):
                        tk = (lambda a, k, tt=tt: a[:, k, tt * 512:(tt + 1) * 512])
                        yield dict(kind="fm", nb=2, mms=[(0, f * 128, 128, tk, 512)],
                                   meta=("g", fb + f, tb * TB + tt * 512))

        def setup(ps, sb):
            cosb = sb(f"{name}cos", [128, S], F32)
            sinb = sb(f"{name}sin", [128, S], F32)
            ta = [sb(f"{name}ta{i}", [128, 512], F32) for i in range(4)]
            sg = [sb(f"{name}sg{i}", [128, 512], F32) for i in range(2)]
            NO = 4
            ob = [sb(f"{name}o{i}", [128, 512], BF16) for i in range(NO)]
            st_done = [kb.sem(f"{name}sd{i}") for i in range(NO)]
            ldc = kb.sem(f"{name}ldc")
            kb.dma("sync", cosb[:], self.c_cos[:, :], inc=ldc)
            ctok = kb.dma("sync", sinb[:], self.c_sin[:, :], inc=ldc)
            sa = kb.sem(f"{name}sa")
            dv = kb.sem(f"{name}dv")
            osl, gsl = Slots(NO), Slots(2)
            tsl = Slots(1)

            def store(dst, obk, tok, ou):
                stok = kb.dma("scalar", dst, ob[obk][:], waits=[tok], inc=st_done[obk])
                osl.release(ou, stok)

            def epi(it, banks, meta, pe_tok):
                if it is None:
                    kb.op("scalar", lambda e: e.nop(), waits=[(s_, s_.n) for s_ in st_done])
                    return
                if meta[0] == "rot":
                    _, t, head, t0 = meta
                    cs = cosb[:, t0:t0 + 512]
                    sn = sinb[:, t0:t0 + 512]
                    tu, _, w = tsl.next()
                    b0, b1 = banks
                    k1 = kb.op("vector", lambda e: e.tensor_tensor(out=ta[0][:], in0=b0, in1=cs, op=ALU.mult),
                               waits=[pe_tok, ctok] + w, inc=dv)
                    k2 = kb.op("vector", lambda e: e.tensor_tensor(out=ta[1][:], in0=b1, in1=sn, op=ALU.mult), inc=dv)
                    k3 = kb.op("vector", lambda e: e.tensor_tensor(out=ta[2][:], in0=b0, in1=sn, op=ALU.mult), inc=dv)
                    k4 = kb.op("vector", lambda e: e.tensor_tensor(out=ta[3][:], in0=b1, in1=cs, op=ALU.mult), inc=dv)
                    ou1, o1, w1 = osl.next()
                    k5 = kb.op("vector", lambda e, o1=o1: e.tensor_tensor(out=ob[o1][:], in0=ta[0][:], in1=ta[1][:], op=ALU.subtract),
                               waits=[k2] + w1, inc=dv)
                    ou2, o2, w2 = osl.next()
                    k6 = kb.op("vector", lambda e, o2=o2: e.tensor_tensor(out=ob[o2][:], in0=ta[2][:], in1=ta[3][:], op=ALU.add),
                               waits=[k4] + w2, inc=dv)
                    tsl.release(tu, k6)
                    dstT = self.rqT if t == 0 else self.rkT
                    store(dstT[head * 256:head * 256 + 128, t0:t0 + 512], o1, k5, ou1)
                    store(dstT[head * 256 + 128:head * 256 + 256, t0:t0 + 512], o2, k6, ou2)
                    return [k4]
                if meta[0] == "v":
                    _, head, cblk = meta
                    ou, obk, w = osl.next()
                    k1 = kb.op("vector", lambda e, obk=obk, bank=banks[0]: e.tensor_copy(out=ob[obk][:], in_=bank),
                               waits=[pe_tok] + w, inc=dv)
                    store(self.rv[head, cblk], obk, k1, ou)
                    return [k1]
                _, fch, t0 = meta
                gu, gb, w = gsl.next()
                a1 = kb.op("scalar", lambda e, gb=gb, bank=banks[0]: e.activation(out=sg[gb][:], in_=bank, func=AF.Sigmoid),
                           waits=[pe_tok] + w, inc=sa)
                ou, obk, w = osl.next()
                k1 = kb.op("vector", lambda e, gb=gb, obk=obk, bank=banks[0]: e.tensor_tensor(
                    out=ob[obk][:], in0=bank, in1=sg[gb][:], op=ALU.mult), waits=[a1] + w, inc=dv)
                gsl.release(gu, k1)
                store(self.rgT[fch * 128:(fch + 1) * 128, t0:t0 + 512], obk, k1, ou)
                return [k1]

            return epi

        self.gemm(name, self.hT, D, self.w_rq, strips, TB, items, setup)

    def phase_ret_core(self):
        kb, nc = self.kb, self.nc
        name = "rc"
        NCH = 32
        QC = 8
        with ExitStack() as ps:
            sb = lambda n, sh, dt: ps.enter_context(nc.sbuf_tensor(n, list(sh), dt))
            decT = sb(f"{name}dec", [128, RH, 128], F32)
            cdB = sb(f"{name}cd", [128, RH, 128], F32)
            sd = sb(f"{name}sdt", [128, RH], F32)
            qb = [sb(f"{name}q{i}", [128, 2, 2, 1024], BF16) for i in range(2)]
            kbf = [sb(f"{name}k{i}", [128, 2, 2, 1024], BF16) for i in range(2)]
            vb = [sb(f"{name}v{i}", [128, 2, QC, 512], BF16) for i in range(2)]
            gb = [sb(f"{name}g{i}", [128, 2, 4, 1024], BF16) for i in range(2)]
            stt = sb(f"{name}st", [128, 2, 2, 512], F32)
            stbf = [sb(f"{name}sb{i}", [128, 2, 2, 512], BF16) for i in range(2)]
            sTd = [sb(f"{name}sT{i}", [128, 128], BF16) for i in range(3)]
            kd = [sb(f"{name}kd{i}", [128, 256], BF16) for i in range(3)]
            qc = [sb(f"{name}qc{i}", [128, 2, 128], BF16) for i in range(3)]
            onb = [sb(f"{name}on{i}", [128, 512], BF16) for i in range(3)]
            junk = sb(f"{name}jk", [128, 512], BF16)
            ssb = [sb(f"{name}ss{i}", [128, 4], F32) for i in range(3)]
            obuf = [sb(f"{name}ob{i}", [128, 2, 4, 512], BF16) for i in range(2)]
            ld = [kb.sem(f"{name}ld{i}") for i in range(2)]
            ldc = kb.sem(f"{name}ldc")
            sp = kb.sem(f"{name}sp")
            sv = kb.sem(f"{name}sv")
            sa = kb.sem(f"{name}sa")
            sg = kb.sem(f"{name}sg")
            st = [kb.sem(f"{name}sto{i}") for i in range(2)]
            kb.dma("sync", decT[:], self.c_decT[:, :, :], inc=ldc)
            kb.dma("sync", cdB[:], self.c_cdB[:, :, :], inc=ldc)
            ctok = kb.dma("sync", sd[:], self.c_sd[:, :], inc=ldc)
            lsl = Slots(2)
            a1sl, usl, osl2, otsl = Slots(2), Slots(1), Slots(2), Slots(2)
            sTsl, kdsl, qcsl, onsl, sssl, obsl = Slots(3), Slots(3), Slots(3), Slots(3), Slots(3), Slots(2)
            steps = [(hp, c, hl) for hp in range(RH // 2) for c in range(NCH) for hl in range(2)]
            units = {}
            X = {}
            state_tok = {}
            cast_tok = {}
            cast_tok2 = {}
            cast_rd = {}
            unit_last = {}

            def load_unit(hp, qq):
                u, slot, w = lsl.next()
                t0 = qq * 1024
                first = True
                for hl in range(2):
                    h = 2 * hp + hl
                    for dst, src in (
                        (qb[slot][:, hl], self.rqT[h * 256:(h + 1) * 256, t0:t0 + 1024].rearrange("(c p) t -> p c t", p=128)),
                        (kbf[slot][:, hl], self.rkT[h * 256:(h + 1) * 256, t0:t0 + 1024].rearrange("(c p) t -> p c t", p=128)),
                        (vb[slot][:, hl], self.rv[h, qq * QC:(qq + 1) * QC].rearrange("c m v -> m c v")),
                        (gb[slot][:, hl], self.rgT[h * 512:(h + 1) * 512, t0:t0 + 1024].rearrange("(c p) t -> p c t", p=128)),
                    ):
                        tok = kb.dma("sync", dst, src, waits=w if first else (), inc=ld[slot])
                        first = False
                units[(hp, qq)] = dict(u=u, slot=slot, tok=tok)

            def s1(t):
                hp, c, hl = steps[t]
                h = 2 * hp + hl
                U = units[(hp, c // QC)]
                slot = U["slot"]
                cl = c % QC
                cs = slice(cl * 128, (cl + 1) * 128)
                au, a1, w = a1sl.next()
                sT = self.psum[:, a1, 0:128]
                kT = self.psum[:, a1, 128:256].bitcast(BF16)
                kb.op("tensor", mm(sT, kbf[slot][:, hl, 0, cs], qb[slot][:, hl, 0, cs], start=True, stop=False),
                      waits=w + [U["tok"], ctok])
                kb.op("tensor", mm(sT, kbf[slot][:, hl, 1, cs], qb[slot][:, hl, 1, cs], start=False, stop=True))
                for kc in range(2):
                    tok = kb.op("tensor", lambda e, kc=kc, kT=kT: e.transpose(kT[:, kc * 128:(kc + 1) * 128],
                                                                              kbf[slot][:, hl, kc, cs], self.ident[:]),
                                inc=sp if kc == 1 else None)
                su, ss_, w = sTsl.next()
                d1 = kb.op("vector", lambda e, ss_=ss_, sT=sT, h=h: e.tensor_tensor(
                    out=sTd[ss_][:], in0=sT, in1=decT[:, h, :], op=ALU.mult), waits=[tok] + w, inc=sv)
                ku, ks, w = kdsl.next()
                d2 = kb.op("scalar", lambda e, ks=ks, kT=kT, h=h: e.activation(
                    out=kd[ks][:], in_=kT, func=AF.Copy, scale=sd[:, h:h + 1]), waits=[tok] + w, inc=sa)
                a1sl.release(au, d1, d2)
                qu, qs, w = qcsl.next()
                for kc in range(2):
                    d3 = kb.op("gpsimd", lambda e, qs=qs, kc=kc, h=h: e.tensor_tensor(
                        out=qc[qs][:, kc, :], in0=qb[slot][:, hl, kc, cs], in1=cdB[:, h, :], op=ALU.mult),
                        waits=(w + [U["tok"], ctok]) if kc == 0 else (), inc=sg)
                X[t] = dict(su=su, ss_=ss_, d1=d1, ku=ku, ks=ks, d2=d2, qu=qu, qs=qs, d3=d3)

            def s2(t):
                hp, c, hl = steps[t]
                h = 2 * hp + hl
                U = units[(hp, c // QC)]
                slot = U["slot"]
                cl = c % QC
                x = X[t]
                uu, _, w = usl.next()
                Ups = self.psum[:, 2:4, :]
                for kc in range(2):
                    tok = kb.op("tensor", mm(Ups[:, kc, :], kd[x["ks"]][:, kc * 128:(kc + 1) * 128], vb[slot][:, hl, cl, :]),
                                waits=([x["d2"]] + w) if kc == 0 else (), inc=sp if kc == 1 else None)
                kdsl.release(x["ku"], tok)
                x["vtok"] = tok
                prev_cast = cast_tok.get((hp, hl, c - 1))
                if c == 0:
                    d = kb.op("vector", lambda e, hl=hl, Ups=Ups: e.tensor_copy(out=stt[:, hl], in_=Ups),
                              waits=[tok] + ([cast_tok[(hp - 1, hl, NCH - 1)]] if hp > 0 else []), inc=sv)
                else:
                    d = kb.op("vector", lambda e, hl=hl, Ups=Ups, h=h: e.scalar_tensor_tensor(
                        out=stt[:, hl], in0=stt[:, hl], scalar=float(GAMMA[h] ** 128), in1=Ups,
                        op0=ALU.mult, op1=ALU.add), waits=[tok, prev_cast], inc=sv)
                usl.release(uu, d)
                par = c % 2
                cw = [cast_rd[(par, hl)]] if (par, hl) in cast_rd else []
                ct = kb.op("scalar", lambda e, par=par, hl=hl: e.activation(out=stbf[par][:, hl], in_=stt[:, hl], func=AF.Copy),
                           waits=[d] + cw, inc=sa)
                cast_tok[(hp, hl, c)] = ct

            def s3(t):
                hp, c, hl = steps[t]
                h = 2 * hp + hl
                U = units[(hp, c // QC)]
                slot = U["slot"]
                cl = c % QC
                x = X[t]
                ou, ob_, w = osl2.next()
                O = self.psum[:, 4 + ob_, :]
                tok = kb.op("tensor", mm(O, sTd[x["ss_"]][:], vb[slot][:, hl, cl, :], start=True, stop=(c == 0)),
                            waits=[x["d1"]] + w, inc=sp if c == 0 else None)
                if c > 0:
                    par = (c - 1) % 2
                    for kc in range(2):
                        tok = kb.op("tensor", mm(O, qc[x["qs"]][:, kc, :], stbf[par][:, hl, kc, :], start=False, stop=(kc == 1)),
                                    waits=[x["d3"], cast_tok[(hp, hl, c - 1)]] if kc == 0 else (), inc=sp if kc == 1 else None)
                    cast_rd[(par, hl)] = tok
                sTsl.release(x["su"], tok)
                qcsl.release(x["qu"], tok)
                x["otok"] = tok
                if c % QC == QC - 1 and hl == 1:
                    unit_last[(hp, c // QC)] = tok
                zu, zs, w = sssl.next()
                z = ssb[zs]
                n1 = kb.op("scalar", lambda e, z=z, O=O: e.activation(out=junk[:], in_=O, func=AF.Square, accum_out=z[:, 0:1]),
                           waits=[tok] + w, inc=sa)
                n2 = kb.op("scalar", lambda e, z=z: e.activation(out=z[:, 1:2], in_=z[:, 0:1], func=AF.Ln, scale=1.0 / 512,
                                                               bias=self.epsc[:, 0:1]), waits=[n1], inc=sa)
                n3 = kb.op("scalar", lambda e, z=z: e.activation(out=z[:, 2:3], in_=z[:, 1:2], func=AF.Exp, scale=-0.5),
                           waits=[n2], inc=sa)
                nu, ns, w = onsl.next()
                n4 = kb.op("scalar", lambda e, z=z, ns=ns, O=O: e.activation(out=onb[ns][:], in_=O, func=AF.Copy, scale=z[:, 2:3]),
                           waits=[n3] + w, inc=sa)
                osl2.release(ou, n4)
                sssl.release(zu, n4)
                x.update(nu=nu, ns=ns, n4=n4)

            def s4(t):
                hp, c, hl = steps[t]
                h = 2 * hp + hl
                U = units[(hp, c // QC)]
                slot = U["slot"]
                cl = c % QC
                x = X.pop(t)
                tu, ts_, w = otsl.next()
                OT = self.psum[:, 6 + ts_, 0:256].bitcast(BF16)
                for dvc in range(4):
                    tok = kb.op("tensor", lambda e, dvc=dvc, OT=OT, ns=x["ns"]: e.transpose(
                        OT[:, dvc * 128:(dvc + 1) * 128], onb[ns][:, dvc * 128:(dvc + 1) * 128], self.ident[:]),
                        waits=([x["n4"]] + w) if dvc == 0 else (), inc=sp if dvc == 3 else None)
                onsl.release(x["nu"], tok)
                sc = c // 4
                key = (hp, sc)
                if key not in obs:
                    bu, bs_, bw = obsl.next()
                    obs[key] = dict(bu=bu, bs_=bs_, bw=bw, first=True, toks=[])
                Ob = obs[key]
                w = Ob["bw"] if Ob["first"] else []
                Ob["first"] = False
                c4 = c % 4
                d = kb.op("vector", lambda e, OT=OT, bs_=Ob["bs_"], c4=c4, hl=hl, slot=slot, cl=cl: e.tensor_tensor(
                    out=obuf[bs_][:, hl, :, c4 * 128:(c4 + 1) * 128],
                    in0=OT.rearrange("p (a n) -> p a n", a=4),
                    in1=gb[slot][:, hl, :, cl * 128:(cl + 1) * 128], op=ALU.mult),
                    waits=[tok, U["tok"]] + w, inc=sv)
                otsl.release(tu, d)
                Ob["toks"].append(d)
                if c4 == 3 and hl == 1:
                    for hl2 in range(2):
                        h2 = 2 * hp + hl2
                        stok = kb.dma("scalar",
                                      self.roT[h2 * 512:(h2 + 1) * 512, sc * 512:(sc + 1) * 512].rearrange("(c p) t -> p c t", p=128),
                                      obuf[Ob["bs_"]][:, hl2], waits=[d] if hl2 == 0 else (), inc=st[Ob["bs_"]])
                    obsl.release(Ob["bu"], stok)
                if c % QC == QC - 1 and hl == 1:
                    lsl.release(U["u"], d, unit_last[(hp, c // QC)], x["vtok"])
                    nxt = unit_order.index((hp, c // QC)) + 2
                    if nxt < len(unit_order):
                        load_unit(*unit_order[nxt])

            obs = {}
            unit_order = [(hp, qq) for hp in range(RH // 2) for qq in range(NCH // QC)]
            load_unit(*unit_order[0])
            load_unit(*unit_order[1])
            T = len(steps)
            for t in range(-1, T + 1):
                if 0 <= t + 1 < T:
                    s1(t + 1)
                if 0 <= t < T:
                    s2(t)
                    s3(t)
                if 0 <= t - 1 < T:
                    s4(t - 1)
            kb.op("scalar", lambda e: e.nop(), waits=[(s_, s_.n) for s_ in st])
            kb.flush()

    def build_all(self):
        self.declare()
        self.phase_mod()
        self.phase_attn_bias()
        self.phase_norm(self.xT, 0, 0, "n0")
        self.phase_qkv()
        self.phase_attn()
        self.phase_resid_gemm("wo", self.oTd, D, self.w_ao, self.xT, self.x1, self.par[:, 0, 2, :], 2048, 512)
        self.phase_norm(self.x1, 0, 1, "n1")
        self.phase_ffn_up("up0", self.w_up[0], side=self.mod_side(1, "m1"))
        self.phase_resid_gemm("dn0", self.hid, DFF, self.w_dn[0], self.x1, self.x2, self.par[:, 0, 5, :], 1024, 256, norm=(1, 0))
        self.phase_ret_gemm()
        self.phase_ret_core()
        self.phase_resid_gemm("ro", self.roT, 2 * D, self.w_ro, self.x2, self.x3, self.par[:, 1, 2, :], 1024, 256, norm=(1, 1))
        self.phase_ffn_up("up1", self.w_up[1])
        self.phase_resid_gemm("dn1", self.hid, DFF, self.w_dn[1], self.x3, self.outT, self.par[:, 1, 5, :], 1024, 256)


def LateTok(tok):
    return tok


def _t5_bucket_np(dist):
    max_exact = 16
    d_f = np.maximum(dist, 1).astype(np.float32)
    large = max_exact + (np.log(d_f / max_exact) / np.float32(math.log(2048 / max_exact)) * (32 - max_exact)).astype(np.int32)
    large = np.minimum(large, 31)
    return np.where(dist < max_exact, dist, large)


def _consts():
    c = {}
    c["c_ident"] = np.eye(128, dtype=np.float32).astype(ml_dtypes.bfloat16)
    c["c_rev"] = np.ascontiguousarray(np.eye(128, dtype=np.float32)[::-1])
    oh = np.zeros((33, 3, 384), np.float32)
    for g, dil in enumerate(DILS):
        for m in range(384):
            delta = m - 127
            if 0 <= delta <= 128:
                b = int(_t5_bucket_np(np.array([delta * dil], np.int32))[0])
                oh[b, g, m] = 1.0
            else:
                oh[32, g, m] = 1.0
    c["c_onehot"] = oh
    half = 128
    inv = (1.0 / (np.float32(10000.0) ** np.linspace(0.0, 1.0, half, dtype=np.float32))).astype(np.float32)
    ang = (np.arange(S, dtype=np.float32)[:, None] * inv[None, :]).astype(np.float32)
    c["c_cos"] = np.ascontiguousarray(np.cos(ang).T.astype(np.float32))
    c["c_sin"] = np.ascontiguousarray(np.sin(ang).T.astype(np.float32))
    m = np.arange(128)[:, None]
    n = np.arange(128)[None, :]
    decT = np.zeros((128, RH, 128), np.float32)
    cdB = np.zeros((128, RH, 128), np.float32)
    sd = np.zeros((128, RH), np.float32)
    for h in range(RH):
        lg = math.log(GAMMA[h])
        decT[:, h, :] = np.where(n >= m, np.exp((n - m) * lg), 0.0) / 16.0
        cdB[:, h, :] = np.exp((np.arange(128) + 1.0) * lg)[None, :]
        sd[:, h] = np.exp((127.0 - np.arange(128)) * lg) / 16.0
    c["c_decT"] = decT
    c["c_cdB"] = cdB
    c["c_sd"] = sd
    return c


def _core_inputs(inp, b, consts):
    f = lambda a: np.ascontiguousarray(np.asarray(a, dtype=np.float32))
    m = {}
    m["xT"] = f(inp["x"][b].T)
    m["cT"] = f(inp["c"][b].reshape(16, 128).T)
    m["w_mod"] = f(inp["w_mod"])
    m["b_modT"] = f(inp["b_mod"].reshape(2, 96, 128).transpose(0, 2, 1))
    m["nmixT"] = f(inp["norm_mix"].reshape(2, 16, 128).transpose(0, 2, 1))
    m["nffnT"] = f(inp["norm_ffn"].reshape(2, 16, 128).transpose(0, 2, 1))
    m["relb"] = f(inp["rel_bias"])
    m["w_qkv"] = f(inp["att_w_qkv"][0])
    m["qgT"] = f(inp["att_q_gain"][0].T)
    m["kgT"] = f(inp["att_k_gain"][0].T)
    m["w_ao"] = f(inp["att_w_o"][0])
    m["w_rq"] = f(inp["ret_w_qkvg"][0])
    m["w_ro"] = f(inp["ret_w_o"][0])
    m["w_up"] = f(inp["ffn_w_up"])
    m["w_dn"] = f(inp["ffn_w_down"])
    m.update(consts)
    return m


def build_nc():
    nc = bass.Bass("TRN2", target_bir_lowering=False)
    P = Prog(nc)
    P.build_all()
    return nc


def kernel(**inputs):
    inp = {k: np.asarray(v) for k, v in inputs.items()}
    nc = build_nc()
    consts = _consts()
    shared = _core_inputs(inp, 0, consts)
    in_maps = []
    for b in range(8):
        m = dict(shared)
        m["xT"] = np.ascontiguousarray(inp["x"][b].T.astype(np.float32))
        m["cT"] = np.ascontiguousarray(inp["c"][b].reshape(16, 128).T.astype(np.float32))
        in_maps.append(m)
    res = run_bass_kernel_spmd(nc, in_maps, core_ids=list(range(8)))
    out = np.stack([np.asarray(r["outT"]).T for r in res.results], axis=0)
    return np.ascontiguousarray(out.astype(np.float32))
```

```python
import math
from contextlib import ExitStack

import numpy as np
import ml_dtypes
import concourse.bass as bass
import concourse.mybir as mybir
from concourse.bass_utils import run_bass_kernel_spmd

F32 = mybir.dt.float32
BF16 = mybir.dt.bfloat16
AF = mybir.ActivationFunctionType
ALU = mybir.AluOpType
ENG = ("sync", "scalar", "vector", "gpsimd", "tensor")

D = 2048
S = 4096
DC = D // 128
DFF = 5632
EPS = 1e-6
NEG = -30000.0
DILS = (1, 4, 16)
RH = 8
GAMMA = [1.0 - 2.0 ** (-5.0 - h) for h in range(RH)]


class Sem:
    def __init__(self, h, name):
        self.h = h
        self.n = 0
        self.name = name


class KB:
    def __init__(self, nc, stack):
        self.nc = nc
        self.stack = stack
        self.ops = {e: [] for e in ENG}
        self.nsem = 0
        self.free_sems = []
        self.phase_sems = []

    def sem(self, name):
        if self.free_sems:
            s = self.free_sems.pop()
        else:
            h = self.stack.enter_context(self.nc.semaphore(f"s{self.nsem}"))
            self.nsem += 1
            s = Sem(h, name)
        s.name = name
        self.phase_sems.append(s)
        return s

    def op(self, eng, fn, waits=(), inc=None, k=None):
        ws = []
        for w in waits:
            if w is None:
                continue
            s, v = w
            if v <= 0:
                continue
            assert v <= s.n, f"wait on future value {s.name}: {v} > {s.n}"
            ws.append((s.h, v))
        tok = None
        ins = None
        if inc is not None:
            if k is None:
                k = 1
            inc.n += k
            ins = (inc.h, k)
            tok = (inc, inc.n)
        self.ops[eng].append((ws, fn, ins))
        return tok

    def dma(self, eng, out, in_, waits=(), inc=None):
        return self.op(eng, lambda e: e.dma_start(out=out, in_=in_), waits=waits, inc=inc, k=16)

    def flush(self):
        ops = self.ops
        self.ops = {e: [] for e in ENG}
        with self.nc.Block() as block:
            for name in ENG:
                lst = ops[name]
                if not lst:
                    continue

                def body(e, lst=lst):
                    for ws, fn, ins in lst:
                        for h, v in ws:
                            e.wait_ge(h, v)
                        i = fn(e)
                        if ins is not None:
                            i.then_inc(ins[0], ins[1])

                getattr(block, name)(body)
        self.free_sems.extend(self.phase_sems)
        self.phase_sems = []


class Slots:
    def __init__(self, n):
        self.n = n
        self.use = 0
        self.rel = {}

    def next(self):
        u = self.use
        self.use += 1
        w = self.rel.get(u - self.n, [])
        return u, u % self.n, list(w)

    def release(self, u, *toks):
        self.rel.setdefault(u, []).extend([t for t in toks if t is not None])


def mm(out, lhsT, rhs, start=True, stop=True):
    return lambda e: e.matmul(out, lhsT, rhs, start=start, stop=stop)


class Prog:
    def __init__(self, nc, debug=None):
        self.nc = nc
        self.debug = debug or {}
        self.gstack = ExitStack()
        self.kb = KB(nc, self.gstack)

    def din(self, name, shape, dt=F32):
        return self.nc.dram_tensor(name, list(shape), dt, kind="ExternalInput").ap()

    def dscr(self, name, shape, dt):
        kind = "ExternalOutput" if name in self.debug.get("outs", ()) else "Internal"
        return self.nc.dram_tensor(name, list(shape), dt, kind=kind).ap()

    def gsb(self, name, shape, dt):
        return self.gstack.enter_context(self.nc.sbuf_tensor(name, list(shape), dt))

    def declare(self):
        d = self.din
        self.xT = d("xT", [D, S])
        self.cT = d("cT", [128, DC])
        self.w_mod = d("w_mod", [2, D, 6 * D])
        self.b_modT = d("b_modT", [2, 128, 96])
        self.nmixT = d("nmixT", [2, 128, DC])
        self.nffnT = d("nffnT", [2, 128, DC])
        self.relb = d("relb", [32, 48])
        self.w_qkv = d("w_qkv", [D, 9 * D])
        self.qgT = d("qgT", [128, 3])
        self.kgT = d("kgT", [128, 3])
        self.w_ao = d("w_ao", [D, D])
        self.w_rq = d("w_rq", [D, 6 * D])
        self.w_ro = d("w_ro", [2 * D, D])
        self.w_up = d("w_up", [2, D, 2 * DFF])
        self.w_dn = d("w_dn", [2, DFF, D])
        self.c_ident = d("c_ident", [128, 128], BF16)
        self.c_onehot = d("c_onehot", [33, 3, 384])
        self.c_cos = d("c_cos", [128, S])
        self.c_sin = d("c_sin", [128, S])
        self.c_decT = d("c_decT", [128, RH, 128])
        self.c_cdB = d("c_cdB", [128, RH, 128])
        self.c_sd = d("c_sd", [128, RH])
        self.c_rev = d("c_rev", [128, 128])
        oname = "outT"
        self.outT = self.nc.dram_tensor(oname, [D, S], F32, kind="ExternalOutput").ap()
        s = self.dscr
        self.hT = s("hT", [D, S], BF16)
        self.qTd = s("qTd", [3, 16, 128, S], BF16)
        self.kTd = s("kTd", [3, 16, 128, S], BF16)
        self.vd = s("vd", [3, 16, 128, 32, 128], BF16)
        self.vecd = s("vecd", [3, 16, 384], F32)
        self.oTd = s("oTd", [D, S], BF16)
        self.x1 = s("x1", [D, S], F32)
        self.x2 = s("x2", [D, S], F32)
        self.x3 = s("x3", [D, S], F32)
        self.hid = s("hid", [DFF, S], BF16)
        self.rqT = s("rqT", [D, S], BF16)
        self.rkT = s("rkT", [D, S], BF16)
        self.rv = s("rv", [RH, 32, 128, 512], BF16)
        self.rgT = s("rgT", [2 * D, S], BF16)
        self.roT = s("roT", [2 * D, S], BF16)
        self.par = self.gsb("par", [128, 2, 6, DC], F32)
        self.ident = self.gsb("ident", [128, 128], BF16)
        self.ones = self.gsb("ones", [128, 128], BF16)
        self.epsc = self.gsb("epsc", [128, 1], F32)
        self.gq = self.gsb("gq", [128, 2, 3], F32)
        self.condb = self.gsb("condb", [128, DC], BF16)
        self.misc = self.gsb("misc", [128, 2, 96 + 2 * DC], F32)
        self.modp = self.gsb("modp", [128, 2, 96], F32)
        self.tmp1 = self.gsb("tmp1", [128, 2, 2, DC], F32)
        self.psum = self.gstack.enter_context(self.nc.psum_tensor("ps", [128, 8, 512], F32))

    def phase_mod(self):
        kb, nc = self.kb, self.nc
        with ExitStack() as ps:
            sb = lambda n, sh, dt: ps.enter_context(nc.sbuf_tensor(n, list(sh), dt))
            cond = sb("cond", [128, DC], F32)
            misc, modp, tmp1 = self.misc, self.modp, self.tmp1
            NST = 3
            stage = [sb(f"mst{i}", [128, DC, 512], F32) for i in range(NST)]
            ld = kb.sem("mod_ld")
            kb.dma("sync", cond[:], self.cT[:, :], inc=ld)
            kb.dma("sync", self.ident[:], self.c_ident[:, :], inc=ld)
            kb.dma("sync", self.gq[:, 0, :], self.qgT[:, :], inc=ld)
            kb.dma("sync", self.gq[:, 1, :], self.kgT[:, :], inc=ld)
            for i in range(2):
                kb.dma("sync", misc[:, i, 0:96], self.b_modT[i], inc=ld)
                kb.dma("sync", misc[:, i, 96:96 + DC], self.nmixT[i], inc=ld)
                kb.dma("sync", misc[:, i, 96 + DC:96 + 2 * DC], self.nffnT[i], inc=ld)
            ldtok = (ld, ld.n)
            c_rdy = kb.sem("mod_c")
            kb.op("gpsimd", lambda e: e.memset(self.ones[:], 1.0), inc=c_rdy)
            kb.op("gpsimd", lambda e: e.memset(self.epsc[:], EPS), inc=c_rdy)
            kb.op("vector", lambda e: e.tensor_scalar(out=self.gq[:, 0, :], in0=self.gq[:, 0, :],
                                                     scalar1=128.0 ** -0.5, scalar2=None, op0=ALU.mult),
                  waits=[ldtok], inc=c_rdy)
            ctok = kb.op("scalar", lambda e: e.activation(out=cond[:], in_=cond[:], func=AF.Silu),
                         waits=[ldtok], inc=c_rdy)
            st_sl = Slots(NST)
            bf_sl = Slots(2)
            st_rdy = [kb.sem(f"mod_sr{i}") for i in range(NST)]
            pe = kb.sem("mod_pe")
            ev = kb.sem("mod_ev")
            mwb = [sb(f"mwb{i}", [128, DC, 512], BF16) for i in range(2)]
            condb = self.condb
            cbtok = kb.op("vector", lambda e: e.tensor_copy(out=condb[:], in_=cond[:]), waits=[ctok], inc=ev)
            for i in range(1):
                pst = self.psum[:, i, 0:96]
                for s_ in range(24):
                    u, slot, w = st_sl.next()
                    for h in range(2):
                        tok = kb.dma("sync", stage[slot][:, h * 8:(h + 1) * 8, :],
                                     self.w_mod[i, h * 1024:(h + 1) * 1024, s_ * 512:(s_ + 1) * 512]
                                     .rearrange("(c p) n -> p c n", p=128),
                                     waits=w if h == 0 else (), inc=st_rdy[slot])
                    bu, bsl, bw = bf_sl.next()
                    c1 = kb.op("vector", lambda e, slot=slot, bsl=bsl: e.tensor_copy(
                        out=mwb[bsl][:, 0:8, :], in_=stage[slot][:, 0:8, :]), waits=[tok] + bw, inc=ev)
                    c2 = kb.op("scalar", lambda e, slot=slot, bsl=bsl: e.activation(
                        out=mwb[bsl][:, 8:16, :], in_=stage[slot][:, 8:16, :], func=AF.Copy), waits=[tok] + bw, inc=c_rdy)
                    st_sl.release(u, c1, c2)
                    for f in range(4):
                        col = s_ * 4 + f
                        for k in range(DC):
                            last = (k == DC - 1)
                            waits = [c1, c2, cbtok] if (f == 0 and k == 0) else []
                            t2 = kb.op("tensor", mm(pst[:, col:col + 1], mwb[bsl][:, k, f * 128:(f + 1) * 128],
                                                    condb[:, k:k + 1], start=(k == 0), stop=last),
                                       waits=waits, inc=pe if (last and f == 3) else None)
                    bf_sl.release(bu, t2)
                self.mod_finish(i, pst, [t2, ldtok], ev)
            kb.flush()

    def mod_finish(self, i, pst, waits, ev):
        kb = self.kb
        misc, modp, tmp1, P_ = self.misc, self.modp, self.tmp1, self.par
        t3 = kb.op("vector", lambda e: e.tensor_tensor(out=modp[:, i, :], in0=pst, in1=misc[:, i, 0:96], op=ALU.add),
                   waits=waits, inc=ev)
        kb.op("vector", lambda e: e.tensor_scalar(out=tmp1[:, i, 0, :], in0=modp[:, i, 16:32], scalar1=1.0, scalar2=None,
                                                 op0=ALU.add), waits=[t3], inc=ev)
        tb = kb.op("vector", lambda e: e.tensor_scalar(out=tmp1[:, i, 1, :], in0=modp[:, i, 64:80], scalar1=1.0, scalar2=None,
                                                      op0=ALU.add), waits=[t3], inc=ev)
        kb.op("vector", lambda e: e.tensor_tensor(out=P_[:, i, 0, :], in0=tmp1[:, i, 0, :], in1=misc[:, i, 96:96 + DC],
                                                 op=ALU.mult), waits=[tb], inc=ev)
        kb.op("vector", lambda e: e.tensor_tensor(out=P_[:, i, 3, :], in0=tmp1[:, i, 1, :],
                                                 in1=misc[:, i, 96 + DC:96 + 2 * DC], op=ALU.mult), waits=[tb], inc=ev)
        for dst, src in ((1, 0), (2, 32), (4, 48), (5, 80)):
            kb.op("vector", lambda e, dst=dst, src=src: e.tensor_copy(out=P_[:, i, dst, :], in_=modp[:, i, src:src + DC]),
                  waits=[t3], inc=ev)

    def mod_side(self, i, name):
        kb = self.kb

        def setup(sb):
            stage = [sb(f"{name}ms{j}", [128, DC, 256], F32) for j in range(2)]
            mwb = [sb(f"{name}mw{j}", [128, DC, 256], BF16) for j in range(2)]
            st_rdy = [kb.sem(f"{name}msr{j}") for j in range(2)]
            sd, sa, spe = kb.sem(f"{name}msd"), kb.sem(f"{name}msa"), kb.sem(f"{name}mpe")
            st_sl, bf_sl = Slots(2), Slots(2)
            pst = self.psum[:, 6, 0:96]
            state = dict(j=0, last=None)
            NS = 6 * D // 256

            def emit_mm():
                p = state.pop("pend", None)
                if p is None:
                    return
                j, bu, bsl, c1, c2 = p
                for f in range(2):
                    col = j * 2 + f
                    for k in range(DC):
                        t2 = kb.op("tensor", mm(pst[:, col:col + 1], mwb[bsl][:, k, f * 128:(f + 1) * 128],
                                                self.condb[:, k:k + 1], start=(k == 0), stop=(k == DC - 1)),
                                   waits=[c1, c2] if (f == 0 and k == 0) else (), inc=spe if (f == 1 and k == DC - 1) else None)
                bf_sl.release(bu, t2)
                state["last"] = t2

            def step():
                emit_mm()
                j = state["j"]
                if j >= NS:
                    return
                state["j"] += 1
                u, slot, w = st_sl.next()
                for h in range(2):
                    tok = kb.dma("sync", stage[slot][:, h * 8:(h + 1) * 8, :],
                                 self.w_mod[i, h * 1024:(h + 1) * 1024, j * 256:(j + 1) * 256]
                                 .rearrange("(c p) n -> p c n", p=128), waits=w if h == 0 else (), inc=st_rdy[slot])
                bu, bsl, bw = bf_sl.next()
                c1 = kb.op("vector", lambda e: e.tensor_copy(out=mwb[bsl][:, 0:8, :], in_=stage[slot][:, 0:8, :]),
                           waits=[tok] + bw, inc=sd)
                c2 = kb.op("scalar", lambda e: e.activation(out=mwb[bsl][:, 8:16, :], in_=stage[slot][:, 8:16, :], func=AF.Copy),
                           waits=[tok] + bw, inc=sa)
                st_sl.release(u, c1, c2)
                state["pend"] = (j, bu, bsl, c1, c2)

            def finish():
                while state["j"] < NS:
                    step()
                emit_mm()
                self.mod_finish(i, pst, [state["last"]], sd)

            return step, finish

        return setup

    def phase_norm(self, xsrc, layer, which, name):
        kb, nc = self.kb, self.nc
        A = self.par[:, layer, 0 if which == 0 else 3, :]
        B = self.par[:, layer, 1 if which == 0 else 4, :]
        TG = 512
        NG = S // TG
        with ExitStack() as ps:
            sb = lambda n, sh, dt: ps.enter_context(nc.sbuf_tensor(n, list(sh), dt))
            xg = [sb(f"{name}x{i}", [128, DC, TG], F32) for i in range(3)]
            sq = [sb(f"{name}q{i}", [128, DC, TG], BF16) for i in range(2)]
            ln = [sb(f"{name}l{i}", [128, TG], F32) for i in range(2)]
            rs = [sb(f"{name}s{i}", [128, TG], F32) for i in range(2)]
            tm = [sb(f"{name}t{i}", [128, TG], F32) for i in range(4)]
            hb = [sb(f"{name}h{i}", [128, DC, TG], BF16) for i in range(2)]
            x_rdy = [kb.sem(f"{name}xr{i}") for i in range(3)]
            sa = kb.sem(f"{name}act")
            sp = kb.sem(f"{name}pe")
            sv = kb.sem(f"{name}dve")
            st = [kb.sem(f"{name}st{i}") for i in range(2)]
            xs, qs, pss, lns, rss, tms, hbs = Slots(3), Slots(2), Slots(2), Slots(2), Slots(2), Slots(4), Slots(2)
            xsrc_r = xsrc.rearrange("(c p) t -> p c t", p=128)
            hT_r = self.hT.rearrange("(c p) t -> p c t", p=128)
            G = {}

            L = {}

            def stage0(g):
                t0 = g * TG
                xu, xb, w = xs.next()
                for h in range(2):
                    xtok = kb.dma("sync", xg[xb][:, h * 8:(h + 1) * 8, :],
                                  xsrc_r[:, h * 8:(h + 1) * 8, t0:t0 + TG], waits=w if h == 0 else (), inc=x_rdy[xb])
                L[g] = (xu, xb, xtok)

            def stage1(g):
                t0 = g * TG
                xu, xb, xtok = L.pop(g)
                qu, qb, w = qs.next()
                sqtok = kb.op("scalar", lambda e, qb=qb, xb=xb: e.activation(out=sq[qb][:], in_=xg[xb][:], func=AF.Square),
                              waits=[xtok] + w, inc=sa)
                pu, pb, w = pss.next()
                pst = self.psum[:, pb, :]
                for c in range(DC):
                    petok = kb.op("tensor", mm(pst, self.ones[:], sq[qb][:, c, :], start=(c == 0), stop=(c == DC - 1)),
                                  waits=([sqtok] + w) if c == 0 else (), inc=sp if c == DC - 1 else None)
                qs.release(qu, petok)
                lu, lb, w = lns.next()
                lntok = kb.op("scalar", lambda e, lb=lb, pst=pst: e.activation(
                    out=ln[lb][:], in_=pst, func=AF.Ln, scale=1.0 / D, bias=self.epsc[:, 0:1]),
                    waits=[petok] + w, inc=sa)
                pss.release(pu, lntok)
                ru, rb, w = rss.next()
                rstok = kb.op("scalar", lambda e, lb=lb, rb=rb: e.activation(
                    out=rs[rb][:], in_=ln[lb][:], func=AF.Exp, scale=-0.5),
                    waits=[lntok] + w, inc=sa)
                lns.release(lu, rstok)
                G[g] = dict(xu=xu, xb=xb, xtok=xtok, sqtok=sqtok, ru=ru, rb=rb, rstok=rstok)

            def stage2(g):
                t0 = g * TG
                I = G.pop(g)
                xu, xb, xtok, sqtok, ru, rb, rstok = I["xu"], I["xb"], I["xtok"], I["sqtok"], I["ru"], I["rb"], I["rstok"]
                hu, hbk, hw = hbs.next()
                for c in range(DC):
                    tu, tbk, w = tms.next()
                    w2 = list(w)
                    if c == 0:
                        w2 += [rstok, xtok]
                    ttok = kb.op("vector", lambda e, xb=xb, rb=rb, c=c, tbk=tbk: e.scalar_tensor_tensor(
                        out=tm[tbk][:], in0=xg[xb][:, c, :], scalar=A[:, c:c + 1], in1=rs[rb][:],
                        op0=ALU.mult, op1=ALU.mult), waits=w2, inc=sv)
                    htok = kb.op("scalar", lambda e, hbk=hbk, c=c, tbk=tbk: e.activation(
                        out=hb[hbk][:, c, :], in_=tm[tbk][:], func=AF.Identity, bias=B[:, c:c + 1], scale=1.0),
                        waits=[ttok] + (hw if c == 0 else []), inc=sa)
                    tms.release(tu, htok)
                xs.release(xu, ttok, sqtok)
                rss.release(ru, ttok)
                sttok = kb.dma("scalar", hT_r[:, :, t0:t0 + TG], hb[hbk][:], waits=[htok], inc=st[hbk])
                hbs.release(hu, sttok)

            stage0(0)
            stage0(1)
            stage1(0)
            for g in range(NG):
                if g + 2 < NG:
                    stage0(g + 2)
                if g + 1 < NG:
                    stage1(g + 1)
                stage2(g)
            kb.op("scalar", lambda e: e.nop(), waits=[(s_, s_.n) for s_ in st])
            kb.flush()

    def gemm(self, name, AT, K, Wd, strips, TB, items_fn, epi_setup, pk=4, nbank=6,
             cast_engs=("gpsimd",), perm=False, side=None, side_every=7):
        kb, nc = self.kb, self.nc
        KC = K // 128
        CS = sum(w for _, w in strips[0])
        NTB = S // TB
        NPIECE = KC // pk
        assert KC % pk == 0
        with ExitStack() as ps:
            sb = lambda n, sh, dt: ps.enter_context(nc.sbuf_tensor(n, list(sh), dt))
            abuf = sb(f"{name}A", [128, KC, TB], BF16)
            pbuf = sb(f"{name}P", [128, KC, TB], BF16) if perm else None
            NST = 3
            stage = [sb(f"{name}S{i}", [128, pk, CS], F32) for i in range(NST)]
            wbf = [sb(f"{name}W{i}", [128, KC, CS], BF16) for i in range(2)]
            epi = epi_setup(ps, sb)
            side_step, side_finish = side(sb) if side is not None else (None, None)
            a_rdy = kb.sem(f"{name}ar")
            st_rdy = [kb.sem(f"{name}sr{i}") for i in range(NST)]
            w_rdy = {(e_, i): kb.sem(f"{name}wr{e_[:1]}{i}") for e_ in ("vector", "scalar", "gpsimd") for i in range(2)}
            pe_done = kb.sem(f"{name}pe")
            pm = kb.sem(f"{name}pm") if perm else None
            npc = [0]
            st_sl = Slots(NST)
            AT_r = AT.rearrange("(c p) t -> p c t", p=128)
            seq = [(tb, s_) for tb in range(NTB) for s_ in range(len(strips))]
            strip_pe_tok = {}
            a_tok = {}
            w_tok = {}
            bank_sl = {}
            nitem = [0]

            def load_A_tile(tb, tt, waits):
                if tb >= NTB or (tb, tt) in a_tok:
                    return
                a_tok[(tb, tt)] = kb.dma("sync", abuf[:, :, tt * 512:(tt + 1) * 512],
                                         AT_r[:, :, tb * TB + tt * 512:tb * TB + (tt + 1) * 512],
                                         waits=waits, inc=a_rdy)

            def load_A(tb):
                if tb >= NTB:
                    return
                waits = []
                if tb >= 1:
                    waits = [strip_pe_tok[tb * len(strips) - 1]]
                for tt in range(TB // 512):
                    load_A_tile(tb, tt, waits)

            def load_W(q):
                if q >= len(seq) or q in w_tok:
                    return
                tb, s_ = seq[q]
                ws = q % 2
                lastc = {}
                for pc in range(NPIECE):
                    u, slot, w = st_sl.next()
                    c0 = 0
                    for j, (col0, width) in enumerate(strips[s_]):
                        tok = kb.dma("sync", stage[slot][:, :, c0:c0 + width],
                                     Wd[pc * pk * 128:(pc + 1) * pk * 128, col0:col0 + width]
                                     .rearrange("(c p) n -> p c n", p=128),
                                     waits=w if j == 0 else (), inc=st_rdy[slot])
                        c0 += width
                    waits = [tok]
                    if q >= 2:
                        waits.append(strip_pe_tok[q - 2])
                    ce = cast_engs[npc[0] % len(cast_engs)]
                    if q == 0:
                        ce = ("vector", "scalar", "gpsimd")[pc % 3]
                    npc[0] += 1
                    if ce == "scalar":
                        cfn = lambda e, slot=slot, ws=ws, pc=pc: e.activation(
                            out=wbf[ws][:, pc * pk:(pc + 1) * pk, :], in_=stage[slot][:], func=AF.Copy)
                    else:
                        cfn = lambda e, slot=slot, ws=ws, pc=pc: e.tensor_copy(
                            out=wbf[ws][:, pc * pk:(pc + 1) * pk, :], in_=stage[slot][:])
                    ctok = kb.op(ce, cfn, waits=waits, inc=w_rdy[(ce, ws)])
                    st_sl.release(u, ctok)
                    lastc[ce] = ctok
                w_tok[q] = list(lastc.values())

            perm_tok = {}

            def perm_ops(tb, s_):
                ns = len(strips) // 3
                if s_ < 8:
                    src, dst, k0, key = abuf, pbuf, 2 * s_, (tb, 1)
                    w0 = [a_tok[(tb, TB // 512 - 1)]] + ([strip_pe_tok[(tb - 1) * len(strips) + 2 * ns - 1]] if tb > 0 else [])
                elif ns <= s_ < ns + 8:
                    src, dst, k0, key = pbuf, abuf, 2 * (s_ - ns), (tb, 2)
                    w0 = [perm_tok[(tb, 1)], strip_pe_tok[tb * len(strips) + ns - 1]]
                else:
                    return
                for k in (k0, k0 + 1):
                    perm_tok[key] = kb.op("scalar", lambda e, src=src, dst=dst, k=k: e.activation(
                        out=dst[:, k, :].rearrange("p (r a) -> p r a", r=4),
                        in_=src[:, k, :].rearrange("p (a r) -> p r a", r=4), func=AF.Copy), waits=w0, inc=pm)

            def perm_wait(tb, s_):
                ns = len(strips) // 3
                if s_ == ns:
                    return [perm_tok[(tb, 1)]]
                if s_ == 2 * ns:
                    return [perm_tok[(tb, 2)]]
                return []

            load_W(0)
            load_A(0)
            for q, (tb, s_) in enumerate(seq):
                cur_abuf = abuf
                if perm:
                    ns_ = len(strips) // 3
                    cur_abuf = pbuf if ns_ <= s_ < 2 * ns_ else abuf
                first = True
                last_tok = None
                its = list(items_fn(tb, s_))
                if all("tt" in it for it in its):
                    its.sort(key=lambda it: it["tt"])
                last_of_tt = {}
                if s_ == len(strips) - 1 and all("tt" in it for it in its):
                    for ii, it in enumerate(its):
                        last_of_tt[it["tt"]] = ii
                for ii, it in enumerate(its):
                    if ii == (0 if len(its) <= 8 else 3):
                        load_W(q + 1)
                        if perm:
                            perm_ops(tb, s_)
                    nbk = it["nb"]
                    sl = bank_sl.setdefault(nbk, Slots(nbank // nbk))
                    u, slot, w = sl.next()
                    banks = [self.psum[:, slot * nbk + j, :] for j in range(nbk)]
                    waits = list(w)
                    if s_ == 0 and "tt" in it:
                        waits.append(a_tok[(tb, it["tt"])])
                    if first and s_ == 0 and "tt" in it:
                        waits += w_tok[q]
                        if perm:
                            waits += perm_wait(tb, s_)
                        first = False
                    if first:
                        waits += w_tok[q] + [a_tok[(tb, TB // 512 - 1)]]
                        if perm:
                            waits += perm_wait(tb, s_)
                        first = False
                    nmm = len(it["mms"])
                    for mi, (bj, wc0, wcs, tokap, ncols_out) in enumerate(it["mms"]):
                        for k in range(KC):
                            wap = wbf[q % 2][:, k, wc0:wc0 + wcs]
                            aap = tokap(cur_abuf, k)
                            last = (mi == nmm - 1 and k == KC - 1)
                            if it["kind"] == "fm":
                                f = mm(banks[bj][:, 0:ncols_out], wap, aap, start=(k == 0), stop=(k == KC - 1))
                            else:
                                f = mm(banks[bj][:, 0:ncols_out], aap, wap, start=(k == 0), stop=(k == KC - 1))
                            tok = kb.op("tensor", f, waits=waits if (mi == 0 and k == 0) else (),
                                        inc=pe_done if last else None)
                    last_tok = tok
                    nitem[0] += 1
                    rel = epi(it, banks, it["meta"], tok)
                    sl.release(u, *rel)
                    if side_step is not None and nitem[0] % side_every == 0:
                        side_step()
                    if last_of_tt and last_of_tt.get(it["tt"]) == ii:
                        load_A_tile(tb + 1, it["tt"], [tok])
                strip_pe_tok[q] = last_tok
                if s_ == len(strips) - 1:
                    load_A(tb + 1)
            if side_finish is not None:
                side_finish()
            epi(None, None, None, None)
            kb.flush()

    def resid_epi(self, name, xsrc, xdst, gcol, TB, norm=None, upi=2):
        kb = self.kb
        NTT = TB // 512

        def setup(ps, sb):
            NX = 4
            xt = [sb(f"{name}x{i}", [128, 512], F32) for i in range(NX)]
            x_rdy = [kb.sem(f"{name}xr{i}") for i in range(NX)]
            st_done = [kb.sem(f"{name}sd{i}") for i in range(NX)]
            dv = kb.sem(f"{name}dv")
            xsl = Slots(NX)
            if norm is not None:
                layer, which = norm
                A = self.par[:, layer, 0 if which == 0 else 3, :]
                B = self.par[:, layer, 1 if which == 0 else 4, :]
                sqb = [sb(f"{name}sq{i}", [128, 512], BF16) for i in range(3)]
                rsb = sb(f"{name}rs", [128, NTT, 512], F32)
                lnb = sb(f"{name}ln", [128, 512], F32)
                xq = sb(f"{name}xq", [128, 8, 512], F32)
                tmb = [sb(f"{name}tm{i}", [128, 512], F32) for i in range(3)]
                hb = sb(f"{name}hb", [128, DC, 512], BF16)
                sa = kb.sem(f"{name}sa")
                spn = kb.sem(f"{name}spn")
                xq_rdy = kb.sem(f"{name}xq")
                hst = kb.sem(f"{name}hst")
                sqsl, tmsl = Slots(3), Slots(3)
                xdst_r = xdst.rearrange("(c p) t -> p c t", p=128)
                hT_r = self.hT.rearrange("(c p) t -> p c t", p=128)
            pend = []
            st = dict(cur_tb=None, nstat={}, stat_last={}, bank_free={}, todo=[], store_tok={},
                      xq_free=[], hb_free=[], rs_free={}, ln_free=[])

            def emit_stat(p):
                tb, tt, sqtok, qu, qs = p
                n = st["nstat"].get((tb, tt), 0)
                w = [sqtok]
                if n == 0:
                    w += st["bank_free"].get(tt, [])
                tok = kb.op("tensor", mm(self.psum[:, 6 + tt, :], self.ones[:], sqb[qs][:],
                                         start=(n == 0), stop=(n == DC - 1)), waits=w, inc=spn)
                sqsl.release(qu, tok)
                st["nstat"][(tb, tt)] = n + 1
                st["stat_last"][(tb, tt)] = tok

            def plan_norm(tb):
                while st["todo"]:
                    run_unit()
                for tt in range(NTT):
                    l1 = kb.op("scalar", lambda e, tt=tt: e.activation(
                        out=lnb[:], in_=self.psum[:, 6 + tt, :], func=AF.Ln, scale=1.0 / D, bias=self.epsc[:, 0:1]),
                        waits=[st["stat_last"][(tb, tt)]] + st["ln_free"], inc=sa)
                    st["bank_free"][tt] = [l1]
                    l2 = kb.op("scalar", lambda e, tt=tt: e.activation(out=rsb[:, tt, :], in_=lnb[:], func=AF.Exp, scale=-0.5),
                               waits=[l1] + st["rs_free"].get(tt, []), inc=sa)
                    st["ln_free"] = [l2]
                    st["rs_tok", tt] = l2
                for tt in range(NTT):
                    for half in range(2):
                        st["todo"].append(("ld", tb, tt, half))
                        for c in range(8):
                            st["todo"].append(("ch", tb, tt, half, c))
                    st["todo"].append(("st", tb, tt))

            def run_unit():
                if not st["todo"]:
                    return
                u = st["todo"].pop(0)
                if False:
                    pass
                elif u[0] == "ld":
                    _, tb, tt, half = u
                    t0 = tb * TB + tt * 512
                    w = [st["store_tok"][(tb, tt, c)] for c in range(half * 8, half * 8 + 8)] + st["xq_free"]
                    st["xq_tok"] = kb.dma("sync", xq[:], xdst_r[:, half * 8:half * 8 + 8, t0:t0 + 512], waits=w, inc=xq_rdy)
                elif u[0] == "ch":
                    _, tb, tt, half, c = u
                    cc = half * 8 + c
                    tu, ts_, w = tmsl.next()
                    w2 = list(w)
                    if c == 0:
                        w2 += [st["xq_tok"], st["rs_tok", tt]]
                    d1 = kb.op("vector", lambda e, c=c, cc=cc, tt=tt, ts_=ts_: e.scalar_tensor_tensor(
                        out=tmb[ts_][:], in0=xq[:, c, :], scalar=A[:, cc:cc + 1], in1=rsb[:, tt, :],
                        op0=ALU.mult, op1=ALU.mult), waits=w2, inc=dv)
                    a1 = kb.op("scalar", lambda e, cc=cc, ts_=ts_: e.activation(
                        out=hb[:, cc, :], in_=tmb[ts_][:], func=AF.Identity, bias=B[:, cc:cc + 1], scale=1.0),
                        waits=[d1] + (st["hb_free"] if cc == 0 else []), inc=sa)
                    tmsl.release(tu, a1)
                    if c == 7:
                        st["xq_free"] = [d1]
                    st["rs_free"][tt] = [d1]
                    st["h_tok"] = a1
                else:
                    _, tb, tt = u
                    t0 = tb * TB + tt * 512
                    st["hb_free"] = [kb.dma("scalar", hT_r[:, :, t0:t0 + 512], hb[:], waits=[st["h_tok"]], inc=hst)]

            def epi(it, banks, meta, pe_tok):
                if it is None:
                    if norm is not None:
                        while pend:
                            emit_stat(pend.pop(0))
                        plan_norm(st["cur_tb"])
                        while st["todo"]:
                            run_unit()
                        kb.op("scalar", lambda e: e.nop(), waits=[(hst, hst.n)])
                    kb.op("scalar", lambda e: e.nop(), waits=[(s_, s_.n) for s_ in st_done])
                    return
                fch, t0 = meta
                tb, tt = t0 // TB, (t0 % TB) // 512
                if norm is not None and st["cur_tb"] is not None and tb != st["cur_tb"]:
                    while pend:
                        emit_stat(pend.pop(0))
                    plan_norm(st["cur_tb"])
                st["cur_tb"] = tb
                u, b, w = xsl.next()
                xtok = kb.dma("sync", xt[b][:], xsrc[fch * 128:(fch + 1) * 128, t0:t0 + 512], waits=w, inc=x_rdy[b])
                dtok = kb.op("vector", lambda e, b=b, fch=fch, bank=banks[0]: e.scalar_tensor_tensor(
                    out=xt[b][:], in0=bank, scalar=gcol[:, fch:fch + 1], in1=xt[b][:],
                    op0=ALU.mult, op1=ALU.add), waits=[pe_tok, xtok], inc=dv)
                stok = kb.dma("scalar", xdst[fch * 128:(fch + 1) * 128, t0:t0 + 512], xt[b][:], waits=[dtok], inc=st_done[b])
                if norm is None:
                    xsl.release(u, stok)
                    return [dtok]
                st["store_tok"][(tb, tt, fch)] = stok
                qu, qs, w = sqsl.next()
                sqtok = kb.op("scalar", lambda e, b=b, qs=qs: e.activation(out=sqb[qs][:], in_=xt[b][:], func=AF.Square),
                              waits=[dtok] + w, inc=sa)
                xsl.release(u, stok, sqtok)
                pend.append((tb, tt, sqtok, qu, qs))
                if len(pend) > 2:
                    emit_stat(pend.pop(0))
                for _ in range(upi):
                    run_unit()
                return [dtok]

            return epi

        return setup

    def resid_items(self, TB, ncols_strip):
        def items(tb, s_):
            for f in range(ncols_strip // 128):
                for tt in range(TB // 512):
                    yield dict(kind="fm", nb=1, tt=tt,
                               mms=[(0, f * 128, 128, (lambda a, k, tt=tt: a[:, k, tt * 512:(tt + 1) * 512]), 512)],
                               meta=(s_ * (ncols_strip // 128) + f, tb * TB + tt * 512))
        return items

    def phase_resid_gemm(self, name, AT, K, Wd, xsrc, xdst, gcol, TB, CS, norm=None, upi=2):
        strips = [[(c0, CS)] for c0 in range(0, D, CS)]
        self.gemm(name, AT, K, Wd, strips, TB, self.resid_items(TB, CS), self.resid_epi(name, xsrc, xdst, gcol, TB, norm, upi),
                  cast_engs=("vector", "gpsimd", "vector"))

    def phase_ffn_up(self, name, Wd, side=None):
        kb = self.kb
        TB = 2048
        strips = [[(256 * j, 256), (DFF + 256 * j, 256)] for j in range(DFF // 256)]
        hid = self.hid

        def items(tb, s_):
            for jj in range(2):
                for tt in range(TB // 512):
                    tk = (lambda a, k, tt=tt: a[:, k, tt * 512:(tt + 1) * 512])
                    yield dict(kind="fm", nb=2, tt=tt,
                               mms=[(0, jj * 128, 128, tk, 512), (1, 256 + jj * 128, 128, tk, 512)],
                               meta=(2 * s_ + jj, tb * TB + tt * 512))

        def setup(ps, sb):
            NB = 3
            sg = [sb(f"{name}g{i}", [128, 512], F32) for i in range(NB)]
            ob = [sb(f"{name}o{i}", [128, 512], BF16) for i in range(NB)]
            st_done = [kb.sem(f"{name}sd{i}") for i in range(NB)]
            sa = kb.sem(f"{name}sa")
            dv = kb.sem(f"{name}dv")
            gsl, osl = Slots(NB), Slots(NB)

            def epi(it, banks, meta, pe_tok):
                if it is None:
                    kb.op("scalar", lambda e: e.nop(), waits=[(s_, s_.n) for s_ in st_done])
                    return
                ch, t0 = meta
                gu, gb, w = gsl.next()
                atok = kb.op("scalar", lambda e, gb=gb, bank=banks[0]: e.activation(out=sg[gb][:], in_=bank, func=AF.Silu),
                             waits=[pe_tok] + w, inc=sa)
                ou, obk, w = osl.next()
                dtok = kb.op("vector", lambda e, gb=gb, obk=obk, bank=banks[1]: e.tensor_tensor(
                    out=ob[obk][:], in0=bank, in1=sg[gb][:], op=ALU.mult), waits=[atok] + w, inc=dv)
                gsl.release(gu, dtok)
                stok = kb.dma("scalar", hid[ch * 128:(ch + 1) * 128, t0:t0 + 512], ob[obk][:], waits=[dtok], inc=st_done[obk])
                osl.release(ou, stok)
                return [dtok]

            return epi

        self.gemm(name, self.hT, D, Wd, strips, TB, items, setup, side=side)

    def phase_attn_bias(self):
        kb, nc = self.kb, self.nc
        with ExitStack() as ps:
            sb = lambda n, sh, dt: ps.enter_context(nc.sbuf_tensor(n, list(sh), dt))
            rb = sb("ab_rb", [33, 48], F32)
            oh = sb("ab_oh", [33, 3, 384], F32)
            vec = sb("ab_vec", [16, 3, 384], F32)
            ld = kb.sem("ab_ld")
            pe = kb.sem("ab_pe")
            dv = kb.sem("ab_dv")
            st = kb.sem("ab_st")
            mtok = kb.op("gpsimd", lambda e: e.memset(rb[32:33, :], NEG), inc=dv)
            kb.dma("sync", rb[0:32, :], self.relb[:, :], inc=ld)
            ltok = kb.dma("sync", oh[:], self.c_onehot[:, :, :], inc=ld)
            for g in range(3):
                pst = self.psum[0:16, g, 0:384]
                ptok = kb.op("tensor", mm(pst, rb[:, g * 16:(g + 1) * 16], oh[:, g, :]), waits=[ltok, mtok], inc=pe)
                ctok = kb.op("vector", lambda e, g=g, pst=pst: e.tensor_copy(out=vec[:, g, :], in_=pst), waits=[ptok], inc=dv)
                kb.dma("scalar", self.vecd[g], vec[:, g, :], waits=[ctok], inc=st)
            kb.op("scalar", lambda e: e.nop(), waits=[(st, st.n)])
            kb.flush()

    @staticmethod
    def grp_tok512(dil, tt):
        if dil == 1:
            return lambda a, k: a[:, k, tt * 512:(tt + 1) * 512]
        if dil == 4:
            return lambda a, k: a[:, k, tt:2048:4]
        return lambda a, k: a[:, k, :].rearrange("p (a r) -> p r a", r=16)[:, 4 * tt:4 * tt + 4, :]

    @staticmethod
    def grp_tok128(dil, blk):
        if dil == 1:
            return lambda a, k: a[:, k, blk * 128:(blk + 1) * 128]
        if dil == 4:
            r, ab = blk // 4, blk % 4
            return lambda a, k: a[:, k, r + 512 * ab:r + 512 * ab + 509:4]
        return lambda a, k: a[:, k, blk:2048:16]

    def phase_qkv(self):
        kb = self.kb
        name = "qkv"
        TB = 2048
        strips = [[(c0, 512)] for c0 in range(0, 9 * D, 512)]

        def items(tb, s_):
            g, t, hq = s_ // 12, (s_ % 12) // 4, s_ % 4
            dil = DILS[g]
            if t < 2:
                for f in range(4):
                    for tt in range(4):
                        yield dict(kind="fm", nb=1, tt=tt, mms=[(0, f * 128, 128, self.grp_tok512(1, tt), 512)],
                                   meta=("qk", g, t, 4 * hq + f, tb * TB + tt * 512))
            else:
                for blk in range(16):
                    yield dict(kind="tm", nb=1, mms=[(0, 0, 512, self.grp_tok128(1, blk), 512)],
                               meta=("v", g, hq, tb * 16 + blk))

        def setup(ps, sb):
            sqb = [sb(f"{name}q{i}", [128, 512], BF16) for i in range(3)]
            lnb = [sb(f"{name}l{i}", [128, 512], F32) for i in range(2)]
            rsb = [sb(f"{name}r{i}", [128, 512], F32) for i in range(2)]
            NO = 4
            ob = [sb(f"{name}o{i}", [128, 512], BF16) for i in range(NO)]
            st_done = [kb.sem(f"{name}sd{i}") for i in range(NO)]
            sa = kb.sem(f"{name}sa")
            dv = kb.sem(f"{name}dv")
            sp = kb.sem(f"{name}sp")
            qsl, lsl, rsl, osl, ssl = Slots(3), Slots(2), Slots(2), Slots(NO), Slots(2)
            pend = []
            cnt = [dv.n]

            def flush_pending():
                if not pend:
                    return
                (bank, meta, qu, qb, sqtok) = pend.pop()
                _, g, t, head, p0 = meta
                su, sbk, w = ssl.next()
                pst = self.psum[:, 6 + sbk, :]
                ptok = kb.op("tensor", mm(pst, self.ones[:], sqb[qb][:]), waits=[sqtok] + w, inc=sp)
                qsl.release(qu, ptok)
                lu, lb, w = lsl.next()
                ltok = kb.op("scalar", lambda e, lb=lb, pst=pst: e.activation(
                    out=lnb[lb][:], in_=pst, func=AF.Ln, scale=1.0 / 128, bias=self.epsc[:, 0:1]),
                    waits=[ptok] + w, inc=sa)
                ssl.release(su, ltok)
                ru, rb, w = rsl.next()
                rtok = kb.op("scalar", lambda e, lb=lb, rb=rb: e.activation(
                    out=rsb[rb][:], in_=lnb[lb][:], func=AF.Exp, scale=-0.5), waits=[ltok] + w, inc=sa)
                lsl.release(lu, rtok)
                ou, obk, w = osl.next()
                dtok = kb.op("vector", lambda e, bank=bank, rb=rb, obk=obk, t=t, g=g: e.scalar_tensor_tensor(
                    out=ob[obk][:], in0=bank, scalar=self.gq[:, t, g:g + 1], in1=rsb[rb][:],
                    op0=ALU.mult, op1=ALU.mult), waits=[rtok] + w, inc=dv)
                rsl.release(ru, dtok)
                dst = (self.qTd if t == 0 else self.kTd)[g, head, :, p0:p0 + 512]
                stok = kb.dma("scalar", dst, ob[obk][:], waits=[dtok], inc=st_done[obk])
                osl.release(ou, stok)

            def epi(it, banks, meta, pe_tok):
                if it is None:
                    flush_pending()
                    kb.op("scalar", lambda e: e.nop(), waits=[(s_, s_.n) for s_ in st_done])
                    return
                cnt[0] += 1
                mytok = (dv, cnt[0])
                if meta[0] == "qk":
                    qu, qb, w = qsl.next()
                    sqtok = kb.op("scalar", lambda e, qb=qb, bank=banks[0]: e.activation(
                        out=sqb[qb][:], in_=bank, func=AF.Square), waits=[pe_tok] + w, inc=sa)
                    flush_pending()
                    pend.append((banks[0], meta, qu, qb, sqtok))
                    return [LateTok(mytok)]
                flush_pending()
                _, g, hq, B = meta
                ou, obk, w = osl.next()
                dtok = kb.op("vector", lambda e, obk=obk, bank=banks[0]: e.tensor_copy(out=ob[obk][:], in_=bank),
                             waits=[pe_tok] + w, inc=dv)
                assert dtok[1] == mytok[1]
                dst = self.vd[g, 4 * hq:4 * hq + 4, :, B, :].rearrange("h p d -> p h d")
                stok = kb.dma("scalar", dst, ob[obk][:].rearrange("p (h d) -> p h d", h=4), waits=[dtok], inc=st_done[obk])
                osl.release(ou, stok)
                return [dtok]

            return epi

        self.gemm(name, self.hT, D, self.w_qkv, strips, TB, items, setup, perm=True,
                  cast_engs=("gpsimd", "vector", "gpsimd"))


    def phase_attn(self):
        kb, nc = self.kb, self.nc
        name = "at"
        with ExitStack() as ps:
            sb = lambda n, sh, dt: ps.enter_context(nc.sbuf_tensor(n, list(sh), dt))
            qb = [sb(f"{name}q{i}", [128, S], BF16) for i in range(2)]
            kbf = [sb(f"{name}k{i}", [128, S], BF16) for i in range(2)]
            vb = [sb(f"{name}v{i}", [128, 32, 128], BF16) for i in range(2)]
            bm = [sb(f"{name}b{i}", [128, 256], F32) for i in range(2)]
            bmr = [sb(f"{name}br{i}", [128, 256], F32) for i in range(2)]
            bmh = [sb(f"{name}bh{i}", [128, 2, 256], BF16) for i in range(2)]
            revm = sb(f"{name}rev", [128, 128], F32)
            ldb = [kb.sem(f"{name}ldb{i}") for i in range(2)]
            ldr = kb.sem(f"{name}ldr")
            rtok = kb.dma("sync", revm[:], self.c_rev[:, :], inc=ldr)
            bm_w = {}
            tb_ = [sb(f"{name}t{i}", [128, 256], F32) for i in range(3)]
            pb = [sb(f"{name}p{i}", [128, 256], BF16) for i in range(6)]
            acc = [sb(f"{name}a{i}", [128, 2, S], F32) for i in range(2)]
            ob = [sb(f"{name}o{i}", [128, S], BF16) for i in range(2)]
            lnd = sb(f"{name}ln", [128, S], F32)
            ld = [kb.sem(f"{name}ld{i}") for i in range(2)]
            sp = kb.sem(f"{name}sp")
            sv = kb.sem(f"{name}sv")
            sa = kb.sem(f"{name}sa")
            spv = kb.sem(f"{name}spv")
            st = [kb.sem(f"{name}st{i}") for i in range(2)]
            lsl, ssl, tsl, psl, odsl, asl, osl = Slots(2), Slots(4), Slots(3), Slots(6), Slots(3), Slots(2), Slots(2)
            units = [(h, g) for h in range(16) for g in range(3)]
            NB = 32
            st_ = {}

            def blocks(g):
                dil = DILS[g]
                nper = 16 // dil
                out = []
                for B in range(NB):
                    tbk, rem = B // 16, B % 16
                    r, ab = rem // nper, rem % nper
                    if ab > 0:
                        prev = B - 1
                    elif tbk == 1:
                        prev = r * nper + nper - 1
                    else:
                        prev = None
                    off = 2048 * tbk + 128 * ab * dil + r
                    out.append((prev, slice(off, off + 127 * dil + 1, dil)))
                return out

            pending_j = []

            def load_unit(ui):
                h, g = units[ui]
                u, slot, w = lsl.next()
                src = bass.AP(self.vecd.tensor, (g * 16 + h) * 384, [[1, 128], [1, 256]])
                htok = kb.dma("sync", bmr[slot][:], src, waits=w, inc=ldb[slot])
                kb.dma("sync", qb[slot][:], self.qTd[g, h], inc=ld[slot])
                kb.dma("sync", kbf[slot][:], self.kTd[g, h], inc=ld[slot])
                tok = kb.dma("sync", vb[slot][:], self.vd[g, h], inc=ld[slot])
                st_[ui] = dict(u=u, slot=slot, tok=tok, bmtok=None, blocks=blocks(g))

                def finish_j():
                    jt = kb.op("tensor", mm(self.psum[:, 7, 0:256], revm[:], bmr[slot][:]),
                               waits=[htok, rtok] + bm_w.get("w", []), inc=sp)
                    b0 = kb.op("scalar", lambda e: e.activation(out=bm[slot][:], in_=self.psum[:, 7, 0:256], func=AF.Copy),
                               waits=[jt], inc=sa)
                    bm_w["w"] = [b0]
                    b1 = kb.op("vector", lambda e: e.tensor_copy(out=bmh[slot][:, 0, :], in_=bm[slot][:]),
                               waits=[b0], inc=sv)
                    st_[ui]["bmtok"] = kb.op("vector", lambda e: e.tensor_tensor(
                        out=bmh[slot][:, 1, :], in0=bm[slot][:], in1=bmh[slot][:, 0, :], op=ALU.subtract), waits=[b1], inc=sv)

                pending_j.append(finish_j)

            seqn = [(ui, B) for ui in range(len(units)) for B in range(NB)]
            info = {}
            acc_state = {}
            lastpv = {}

            def stageA(n):
                ui, B = seqn[n]
                U = st_[ui]
                slot = U["slot"]
                prev, csl = U["blocks"][B]
                wd = 256 if prev is not None else 128
                su, sslot, w = ssl.next()
                S_ = self.psum[:, sslot, 0:256]
                bs = slice(B * 128, (B + 1) * 128)
                kb.op("tensor", mm(S_[:, 0:wd], self.ident[:], bmh[slot][:, 0, 0:wd], start=True, stop=False),
                      waits=w + ([U["tok"], U["bmtok"]] if B == 0 else []))
                kb.op("tensor", mm(S_[:, 0:wd], self.ident[:], bmh[slot][:, 1, 0:wd], start=False, stop=False))
                tok = kb.op("tensor", mm(S_[:, 0:128], kbf[slot][:, bs], qb[slot][:, bs], start=False, stop=(prev is None)),
                            inc=sp if prev is None else None)
                if prev is not None:
                    pbs = slice(prev * 128, (prev + 1) * 128)
                    tok = kb.op("tensor", mm(S_[:, 128:256], kbf[slot][:, pbs], qb[slot][:, bs], start=False, stop=True), inc=sp)
                pu, pslot, w = psl.next()
                ptok = kb.op("scalar", lambda e, S_=S_, pslot=pslot, wd=wd: e.activation(
                    out=pb[pslot][:, 0:wd], in_=S_[:, 0:wd], func=AF.Exp), waits=[tok] + w, inc=sa)
                ssl.release(su, ptok)
                info[n] = dict(pu=pu, pslot=pslot, ptok=ptok, prev=prev, csl=csl, wd=wd)

            def stageB(n):
                ui, B = seqn[n]
                h, g = units[ui]
                U = st_[ui]
                slot = U["slot"]
                I = info.pop(n)
                prev, csl, pslot = I["prev"], I["csl"], I["pslot"]
                ou, oslot, w = odsl.next()
                OD = self.psum[:, 4 + oslot, 0:256]
                P_ = pb[pslot]
                kb.op("tensor", mm(OD[:, 0:128], vb[slot][:, B, :], P_[:, 0:128], start=True, stop=(prev is None)),
                      waits=[I["ptok"]] + w)
                if prev is not None:
                    kb.op("tensor", mm(OD[:, 0:128], vb[slot][:, prev, :], P_[:, 128:256], start=False, stop=True))
                tok = kb.op("tensor", mm(OD[:, 128:256], self.ones[:], P_[:, 0:128], start=True, stop=(prev is None)),
                            inc=spv if prev is None else None)
                if prev is not None:
                    tok = kb.op("tensor", mm(OD[:, 128:256], self.ones[:], P_[:, 128:256], start=False, stop=True), inc=spv)
                psl.release(I["pu"], tok)
                lastpv[ui] = tok
                if g == 0 and B == 0:
                    au, aslot, aw = asl.next()
                    acc_state[h] = dict(au=au, aslot=aslot, aw=aw, toks={0: [], 1: [], 2: []})
                A = acc_state[h]
                dst = acc[A["aslot"]][:, :, csl]
                src = OD.rearrange("p (a n) -> p a n", a=2)
                if g == 0:
                    etok = kb.op("scalar", lambda e, dst=dst, src=src: e.activation(out=dst, in_=src, func=AF.Copy),
                                 waits=[tok] + (A["aw"] if B == 0 else []), inc=sa)
                else:
                    wprev = [A["toks"][g - 1][-1]] if B == 0 else []
                    etok = kb.op("vector", lambda e, dst=dst, src=src: e.tensor_tensor(
                        out=dst, in0=src, in1=dst, op=ALU.add), waits=[tok] + wprev, inc=sv)
                A["toks"][g].append(etok)
                odsl.release(ou, etok)
                if B == NB - 1:
                    lsl.release(U["u"], tok)
                    if g == 2:
                        finalize(h)

            fin_q = []

            def finalize(h):
                A = acc_state.pop(h)
                a = acc[A["aslot"]]
                ou, oslot, w = osl.next()
                last = A["toks"][2][-1]
                for ci in range(8):
                    cs = slice(ci * 512, (ci + 1) * 512)

                    def chunk(ci=ci, cs=cs):
                        t1 = kb.op("scalar", lambda e: e.activation(out=lnd[:, cs], in_=a[:, 1, cs], func=AF.Ln),
                                   waits=[last] + fin_w.get(ci, []), inc=sa)
                        t2 = kb.op("scalar", lambda e: e.activation(out=a[:, 1, cs], in_=lnd[:, cs], func=AF.Exp, scale=-1.0),
                                   waits=[t1], inc=sa)
                        fin_w[ci] = [t2]
                        t3 = kb.op("vector", lambda e: e.tensor_tensor(
                            out=ob[oslot][:, cs], in0=a[:, 0, cs], in1=a[:, 1, cs], op=ALU.mult),
                            waits=[t2] + (w if ci == 0 else []), inc=sv)
                        if ci == 7:
                            asl.release(A["au"], t3)
                            stok = kb.dma("scalar", self.oTd[h * 128:(h + 1) * 128, :], ob[oslot][:], waits=[t3], inc=st[oslot])
                            osl.release(ou, stok)

                    fin_q.append(chunk)

            fin_w = {}
            N = len(seqn)
            load_unit(0)
            load_unit(1)
            while pending_j:
                pending_j.pop(0)()
            LOOK = 3
            for n in range(-LOOK, N):
                m = n + LOOK
                if m < N:
                    stageA(m)
                if n >= 0:
                    stageB(n)
                    ui, B = seqn[n]
                    if B == NB - 1 and ui + 2 < len(units):
                        load_unit(ui + 2)
                    if B == 3:
                        while pending_j:
                            pending_j.pop(0)()
                    if fin_q:
                        fin_q.pop(0)()
            while fin_q:
                fin_q.pop(0)()
            kb.op("scalar", lambda e: e.nop(), waits=[(s_, s_.n) for s_ in st])
            kb.flush()


    def phase_ret_gemm(self):
        kb, nc = self.kb, self.nc
        name = "rg"
        TB = 2048
        strips = [[(c0, 512)] for c0 in range(0, 6 * D, 512)]
        cosb = self.gsb_phase = None

        def items(tb, s_):
            c0 = s_ * 512
            if c0 < 2 * D:
                t = 0 if c0 < D else 1
                hbase = ((c0 % D) // 256)
                for hl in range(2):
                    for tt in range(4):
                        tk = (lambda a, k, tt=tt: a[:, k, tt * 512:(tt + 1) * 512])
                        yield dict(kind="fm", nb=2, tt=tt,
                                   mms=[(0, hl * 256, 128, tk, 512), (1, hl * 256 + 128, 128, tk, 512)],
                                   meta=("rot", t, hbase + hl, tb * TB + tt * 512))
            elif c0 < 4 * D:
                head = (c0 - 2 * D) // 512
                for blk in range(16):
                    yield dict(kind="tm", nb=2,
                               mms=[(0, 0, 512, (lambda a, k, blk=blk: a[:, k, blk * 128:(blk + 1) * 128]), 512)],
                               meta=("v", head, tb * 16 + blk))
            else:
                fb = (c0 - 4 * D) // 128
                for f in range(4):
                    for tt in range(4):
                        tk = (lambda a, k, tt=tt: a[:, k, tt * 512:(tt + 1) * 512])
                        yield dict(kind="fm", nb=2, mms=[(0, f * 128, 128, tk, 512)],
                                   meta=("g", fb + f, tb * TB + tt * 512))

        def setup(ps, sb):
            cosb = sb(f"{name}cos", [128, S], F32)
            sinb = sb(f"{name}sin", [128, S], F32)
            ta = [sb(f"{name}ta{i}", [128, 512], F32) for i in range(4)]
            sg = [sb(f"{name}sg{i}", [128, 512], F32) for i in range(2)]
            NO = 4
            ob = [sb(f"{name}o{i}", [128, 512], BF16) for i in range(NO)]
            st_done = [kb.sem(f"{name}sd{i}") for i in range(NO)]
            ldc = kb.sem(f"{name}ldc")
            kb.dma("sync", cosb[:], self.c_cos[:, :], inc=ldc)
            ctok = kb.dma("sync", sinb[:], self.c_sin[:, :], inc=ldc)
            sa = kb.sem(f"{name}sa")
            dv = kb.sem(f"{name}dv")
            osl, gsl = Slots(NO), Slots(2)
            tsl = Slots(1)

            def store(dst, obk, tok, ou):
                stok = kb.dma("scalar", dst, ob[obk][:], waits=[tok], inc=st_done[obk])
                osl.release(ou, stok)

            def epi(it, banks, meta, pe_tok):
                if it is None:
                    kb.op("scalar", lambda e: e.nop(), waits=[(s_, s_.n) for s_ in st_done])
                    return
                if meta[0] == "rot":
                    _, t, head, t0 = meta
                    cs = cosb[:, t0:t0 + 512]
                    sn = sinb[:, t0:t0 + 512]
                    tu, _, w = tsl.next()
                    b0, b1 = banks
                    k1 = kb.op("vector", lambda e: e.tensor_tensor(out=ta[0][:], in0=b0, in1=cs, op=ALU.mult),
                               waits=[pe_tok, ctok] + w, inc=dv)
                    k2 = kb.op("vector", lambda e: e.tensor_tensor(out=ta[1][:], in0=b1, in1=sn, op=ALU.mult), inc=dv)
                    k3 = kb.op("vector", lambda e: e.tensor_tensor(out=ta[2][:], in0=b0, in1=sn, op=ALU.mult), inc=dv)
                    k4 = kb.op("vector", lambda e: e.tensor_tensor(out=ta[3][:], in0=b1, in1=cs, op=ALU.mult), inc=dv)
                    ou1, o1, w1 = osl.next()
                    k5 = kb.op("vector", lambda e, o1=o1: e.tensor_tensor(out=ob[o1][:], in0=ta[0][:], in1=ta[1][:], op=ALU.subtract),
                               waits=[k2] + w1, inc=dv)
                    ou2, o2, w2 = osl.next()
                    k6 = kb.op("vector", lambda e, o2=o2: e.tensor_tensor(out=ob[o2][:], in0=ta[2][:], in1=ta[3][:], op=ALU.add),
                               waits=[k4] + w2, inc=dv)
                    tsl.release(tu, k6)
                    dstT = self.rqT if t == 0 else self.rkT
                    store(dstT[head * 256:head * 256 + 128, t0:t0 + 512], o1, k5, ou1)
                    store(dstT[head * 256 + 128:head * 256 + 256, t0:t0 + 512], o2, k6, ou2)
                    return [k4]
                if meta[0] == "v":
                    _, head, cblk = meta
                    ou, obk, w = osl.next()
                    k1 = kb.op("vector", lambda e, obk=obk, bank=banks[0]: e.tensor_copy(out=ob[obk][:], in_=bank),
                               waits=[pe_tok] + w, inc=dv)
                    store(self.rv[head, cblk], obk, k1, ou)
                    return [k1]
                _, fch, t0 = meta
                gu, gb, w = gsl.next()
                a1 = kb.op("scalar", lambda e, gb=gb, bank=banks[0]: e.activation(out=sg[gb][:], in_=bank, func=AF.Sigmoid),
                           waits=[pe_tok] + w, inc=sa)
                ou, obk, w = osl.next()
                k1 = kb.op("vector", lambda e, gb=gb, obk=obk, bank=banks[0]: e.tensor_tensor(
                    out=ob[obk][:], in0=bank, in1=sg[gb][:], op=ALU.mult), waits=[a1] + w, inc=dv)
                gsl.release(gu, k1)
                store(self.rgT[fch * 128:(fch + 1) * 128, t0:t0 + 512], obk, k1, ou)
                return [k1]

            return epi

        self.gemm(name, self.hT, D, self.w_rq, strips, TB, items, setup)

    def phase_ret_core(self):
        kb, nc = self.kb, self.nc
        name = "rc"
        NCH = 32
        QC = 8
        with ExitStack() as ps:
            sb = lambda n, sh, dt: ps.enter_context(nc.sbuf_tensor(n, list(sh), dt))
            decT = sb(f"{name}dec", [128, RH, 128], F32)
            cdB = sb(f"{name}cd", [128, RH, 128], F32)
            sd = sb(f"{name}sdt", [128, RH], F32)
            qb = [sb(f"{name}q{i}", [128, 2, 2, 1024], BF16) for i in range(2)]
            kbf = [sb(f"{name}k{i}", [128, 2, 2, 1024], BF16) for i in range(2)]
            vb = [sb(f"{name}v{i}", [128, 2, QC, 512], BF16) for i in range(2)]
            gb = [sb(f"{name}g{i}", [128, 2, 4, 1024], BF16) for i in range(2)]
            stt = sb(f"{name}st", [128, 2, 2, 512], F32)
            stbf = [sb(f"{name}sb{i}", [128, 2, 2, 512], BF16) for i in range(2)]
            sTd = [sb(f"{name}sT{i}", [128, 128], BF16) for i in range(3)]
            kd = [sb(f"{name}kd{i}", [128, 256], BF16) for i in range(3)]
            qc = [sb(f"{name}qc{i}", [128, 2, 128], BF16) for i in range(3)]
            onb = [sb(f"{name}on{i}", [128, 512], BF16) for i in range(3)]
            junk = sb(f"{name}jk", [128, 512], BF16)
            ssb = [sb(f"{name}ss{i}", [128, 4], F32) for i in range(3)]
            obuf = [sb(f"{name}ob{i}", [128, 2, 4, 512], BF16) for i in range(2)]
            ld = [kb.sem(f"{name}ld{i}") for i in range(2)]
            ldc = kb.sem(f"{name}ldc")
            sp = kb.sem(f"{name}sp")
            sv = kb.sem(f"{name}sv")
            sa = kb.sem(f"{name}sa")
            sg = kb.sem(f"{name}sg")
            st = [kb.sem(f"{name}sto{i}") for i in range(2)]
            kb.dma("sync", decT[:], self.c_decT[:, :, :], inc=ldc)
            kb.dma("sync", cdB[:], self.c_cdB[:, :, :], inc=ldc)
            ctok = kb.dma("sync", sd[:], self.c_sd[:, :], inc=ldc)
            lsl = Slots(2)
            a1sl, usl, osl2, otsl = Slots(2), Slots(1), Slots(2), Slots(2)
            sTsl, kdsl, qcsl, onsl, sssl, obsl = Slots(3), Slots(3), Slots(3), Slots(3), Slots(3), Slots(2)
            steps = [(hp, c, hl) for hp in range(RH // 2) for c in range(NCH) for hl in range(2)]
            units = {}
            X = {}
            state_tok = {}
            cast_tok = {}
            cast_tok2 = {}
            cast_rd = {}
            unit_last = {}

            def load_unit(hp, qq):
                u, slot, w = lsl.next()
                t0 = qq * 1024
                first = True
                for hl in range(2):
                    h = 2 * hp + hl
                    for dst, src in (
                        (qb[slot][:, hl], self.rqT[h * 256:(h + 1) * 256, t0:t0 + 1024].rearrange("(c p) t -> p c t", p=128)),
                        (kbf[slot][:, hl], self.rkT[h * 256:(h + 1) * 256, t0:t0 + 1024].rearrange("(c p) t -> p c t", p=128)),
                        (vb[slot][:, hl], self.rv[h, qq * QC:(qq + 1) * QC].rearrange("c m v -> m c v")),
                        (gb[slot][:, hl], self.rgT[h * 512:(h + 1) * 512, t0:t0 + 1024].rearrange("(c p) t -> p c t", p=128)),
                    ):
                        tok = kb.dma("sync", dst, src, waits=w if first else (), inc=ld[slot])
                        first = False
                units[(hp, qq)] = dict(u=u, slot=slot, tok=tok)

            def s1(t):
                hp, c, hl = steps[t]
                h = 2 * hp + hl
                U = units[(hp, c // QC)]
                slot = U["slot"]
                cl = c % QC
                cs = slice(cl * 128, (cl + 1) * 128)
                au, a1, w = a1sl.next()
                sT = self.psum[:, a1, 0:128]
                kT = self.psum[:, a1, 128:256].bitcast(BF16)
                kb.op("tensor", mm(sT, kbf[slot][:, hl, 0, cs], qb[slot][:, hl, 0, cs], start=True, stop=False),
                      waits=w + [U["tok"], ctok])
                kb.op("tensor", mm(sT, kbf[slot][:, hl, 1, cs], qb[slot][:, hl, 1, cs], start=False, stop=True))
                for kc in range(2):
                    tok = kb.op("tensor", lambda e, kc=kc, kT=kT: e.transpose(kT[:, kc * 128:(kc + 1) * 128],
                                                                              kbf[slot][:, hl, kc, cs], self.ident[:]),
                                inc=sp if kc == 1 else None)
                su, ss_, w = sTsl.next()
                d1 = kb.op("vector", lambda e, ss_=ss_, sT=sT, h=h: e.tensor_tensor(
                    out=sTd[ss_][:], in0=sT, in1=decT[:, h, :], op=ALU.mult), waits=[tok] + w, inc=sv)
                ku, ks, w = kdsl.next()
                d2 = kb.op("vector", lambda e, ks=ks, kT=kT, h=h: e.tensor_scalar(
                    out=kd[ks][:], in0=kT, scalar1=sd[:, h:h + 1], scalar2=None, op0=ALU.mult), waits=[tok] + w, inc=sv)
                a1sl.release(au, d1, d2)
                qu, qs, w = qcsl.next()
                for kc in range(2):
                    d3 = kb.op("gpsimd", lambda e, qs=qs, kc=kc, h=h: e.tensor_tensor(
                        out=qc[qs][:, kc, :], in0=qb[slot][:, hl, kc, cs], in1=cdB[:, h, :], op=ALU.mult),
                        waits=(w + [U["tok"], ctok]) if kc == 0 else (), inc=sg)
                X[t] = dict(su=su, ss_=ss_, d1=d1, ku=ku, ks=ks, d2=d2, qu=qu, qs=qs, d3=d3)

            def s2(t):
                hp, c, hl = steps[t]
                h = 2 * hp + hl
                U = units[(hp, c // QC)]
                slot = U["slot"]
                cl = c % QC
                x = X[t]
                uu, _, w = usl.next()
                Ups = self.psum[:, 2:4, :]
                for kc in range(2):
                    tok = kb.op("tensor", mm(Ups[:, kc, :], kd[x["ks"]][:, kc * 128:(kc + 1) * 128], vb[slot][:, hl, cl, :]),
                                waits=([x["d2"]] + w) if kc == 0 else (), inc=sp if kc == 1 else None)
                kdsl.release(x["ku"], tok)
                x["vtok"] = tok
                prev_cast = cast_tok.get((hp, hl, c - 1))
                if c == 0:
                    d = kb.op("vector", lambda e, hl=hl, Ups=Ups: e.tensor_copy(out=stt[:, hl], in_=Ups),
                              waits=[tok] + ([cast_tok[(hp - 1, hl, NCH - 1)]] if hp > 0 else []), inc=sv)
                else:
                    d = kb.op("vector", lambda e, hl=hl, Ups=Ups, h=h: e.scalar_tensor_tensor(
                        out=stt[:, hl], in0=stt[:, hl], scalar=float(GAMMA[h] ** 128), in1=Ups,
                        op0=ALU.mult, op1=ALU.add), waits=[tok, prev_cast], inc=sv)
                usl.release(uu, d)
                par = c % 2
                cw = [cast_rd[(par, hl)]] if (par, hl) in cast_rd else []
                ct = kb.op("scalar", lambda e, par=par, hl=hl: e.activation(out=stbf[par][:, hl], in_=stt[:, hl], func=AF.Copy),
                           waits=[d] + cw, inc=sa)
                cast_tok[(hp, hl, c)] = ct

            def s3(t):
                hp, c, hl = steps[t]
                h = 2 * hp + hl
                U = units[(hp, c // QC)]
                slot = U["slot"]
                cl = c % QC
                x = X[t]
                ou, ob_, w = osl2.next()
                O = self.psum[:, 4 + ob_, :]
                tok = kb.op("tensor", mm(O, sTd[x["ss_"]][:], vb[slot][:, hl, cl, :], start=True, stop=(c == 0)),
                            waits=[x["d1"]] + w, inc=sp if c == 0 else None)
                if c > 0:
                    par = (c - 1) % 2
                    for kc in range(2):
                        tok = kb.op("tensor", mm(O, qc[x["qs"]][:, kc, :], stbf[par][:, hl, kc, :], start=False, stop=(kc == 1)),
                                    waits=[x["d3"], cast_tok[(hp, hl, c - 1)]] if kc == 0 else (), inc=sp if kc == 1 else None)
                    cast_rd[(par, hl)] = tok
                sTsl.release(x["su"], tok)
                qcsl.release(x["qu"], tok)
                x["otok"] = tok
                if c % QC == QC - 1 and hl == 1:
                    unit_last[(hp, c // QC)] = tok
                zu, zs, w = sssl.next()
                z = ssb[zs]
                n1 = kb.op("scalar", lambda e, z=z, O=O: e.activation(out=junk[:], in_=O, func=AF.Square, accum_out=z[:, 0:1]),
                           waits=[tok] + w, inc=sa)
                n2 = kb.op("scalar", lambda e, z=z: e.activation(out=z[:, 1:2], in_=z[:, 0:1], func=AF.Ln, scale=1.0 / 512,
                                                               bias=self.epsc[:, 0:1]), waits=[n1], inc=sa)
                n3 = kb.op("scalar", lambda e, z=z: e.activation(out=z[:, 2:3], in_=z[:, 1:2], func=AF.Exp, scale=-0.5),
                           waits=[n2], inc=sa)
                nu, ns, w = onsl.next()
                n4 = kb.op("scalar", lambda e, z=z, ns=ns, O=O: e.activation(out=onb[ns][:], in_=O, func=AF.Copy, scale=z[:, 2:3]),
                           waits=[n3] + w, inc=sa)
                osl2.release(ou, n4)
                sssl.release(zu, n4)
                x.update(nu=nu, ns=ns, n4=n4)

            def s4(t):
                hp, c, hl = steps[t]
                h = 2 * hp + hl
                U = units[(hp, c // QC)]
                slot = U["slot"]
                cl = c % QC
                x = X.pop(t)
                tu, ts_, w = otsl.next()
                OT = self.psum[:, 6 + ts_, 0:256].bitcast(BF16)
                for dvc in range(4):
                    tok = kb.op("tensor", lambda e, dvc=dvc, OT=OT, ns=x["ns"]: e.transpose(
                        OT[:, dvc * 128:(dvc + 1) * 128], onb[ns][:, dvc * 128:(dvc + 1) * 128], self.ident[:]),
                        waits=([x["n4"]] + w) if dvc == 0 else (), inc=sp if dvc == 3 else None)
                onsl.release(x["nu"], tok)
                sc = c // 4
                key = (hp, sc)
                if key not in obs:
                    bu, bs_, bw = obsl.next()
                    obs[key] = dict(bu=bu, bs_=bs_, bw=bw, first=True, toks=[])
                Ob = obs[key]
                w = Ob["bw"] if Ob["first"] else []
                Ob["first"] = False
                c4 = c % 4
                d = kb.op("vector", lambda e, OT=OT, bs_=Ob["bs_"], c4=c4, hl=hl, slot=slot, cl=cl: e.tensor_tensor(
                    out=obuf[bs_][:, hl, :, c4 * 128:(c4 + 1) * 128],
                    in0=OT.rearrange("p (a n) -> p a n", a=4),
                    in1=gb[slot][:, hl, :, cl * 128:(cl + 1) * 128], op=ALU.mult),
                    waits=[tok, U["tok"]] + w, inc=sv)
                otsl.release(tu, d)
                Ob["toks"].append(d)
                if c4 == 3 and hl == 1:
                    for hl2 in range(2):
                        h2 = 2 * hp + hl2
                        stok = kb.dma("scalar",
                                      self.roT[h2 * 512:(h2 + 1) * 512, sc * 512:(sc + 1) * 512].rearrange("(c p) t -> p c t", p=128),
                                      obuf[Ob["bs_"]][:, hl2], waits=[d] if hl2 == 0 else (), inc=st[Ob["bs_"]])
                    obsl.release(Ob["bu"], stok)
                if c % QC == QC - 1 and hl == 1:
                    lsl.release(U["u"], d, unit_last[(hp, c // QC)], x["vtok"])
                    nxt = unit_order.index((hp, c // QC)) + 2
                    if nxt < len(unit_order):
                        load_unit(*unit_order[nxt])

            obs = {}
            unit_order = [(hp, qq) for hp in range(RH // 2) for qq in range(NCH // QC)]
            load_unit(*unit_order[0])
            load_unit(*unit_order[1])
            T = len(steps)
            for t in range(-1, T + 1):
                if 0 <= t + 1 < T:
                    s1(t + 1)
                if 0 <= t < T:
                    s2(t)
                    s3(t)
                if 0 <= t - 1 < T:
                    s4(t - 1)
            kb.op("scalar", lambda e: e.nop(), waits=[(s_, s_.n) for s_ in st])
            kb.flush()

    def build_all(self):
        self.declare()
        self.phase_mod()
        self.phase_attn_bias()
        self.phase_norm(self.xT, 0, 0, "n0")
        self.phase_qkv()
        self.phase_attn()
        self.phase_resid_gemm("wo", self.oTd, D, self.w_ao, self.xT, self.x1, self.par[:, 0, 2, :], 2048, 512)
        self.phase_norm(self.x1, 0, 1, "n1")
        self.phase_ffn_up("up0", self.w_up[0], side=self.mod_side(1, "m1"))
        self.phase_resid_gemm("dn0", self.hid, DFF, self.w_dn[0], self.x1, self.x2, self.par[:, 0, 5, :], 1024, 256, norm=(1, 0))
        self.phase_ret_gemm()
        self.phase_ret_core()
        self.phase_resid_gemm("ro", self.roT, 2 * D, self.w_ro, self.x2, self.x3, self.par[:, 1, 2, :], 1024, 256, norm=(1, 1))
        self.phase_ffn_up("up1", self.w_up[1])
        self.phase_resid_gemm("dn1", self.hid, DFF, self.w_dn[1], self.x3, self.outT, self.par[:, 1, 5, :], 1024, 256)


def LateTok(tok):
    return tok


def _t5_bucket_np(dist):
    max_exact = 16
    d_f = np.maximum(dist, 1).astype(np.float32)
    large = max_exact + (np.log(d_f / max_exact) / np.float32(math.log(2048 / max_exact)) * (32 - max_exact)).astype(np.int32)
    large = np.minimum(large, 31)
    return np.where(dist < max_exact, dist, large)


def _consts():
    c = {}
    c["c_ident"] = np.eye(128, dtype=np.float32).astype(ml_dtypes.bfloat16)
    c["c_rev"] = np.ascontiguousarray(np.eye(128, dtype=np.float32)[::-1])
    oh = np.zeros((33, 3, 384), np.float32)
    for g, dil in enumerate(DILS):
        for m in range(384):
            delta = m - 127
            if 0 <= delta <= 128:
                b = int(_t5_bucket_np(np.array([delta * dil], np.int32))[0])
                oh[b, g, m] = 1.0
            else:
                oh[32, g, m] = 1.0
    c["c_onehot"] = oh
    half = 128
    inv = (1.0 / (np.float32(10000.0) ** np.linspace(0.0, 1.0, half, dtype=np.float32))).astype(np.float32)
    ang = (np.arange(S, dtype=np.float32)[:, None] * inv[None, :]).astype(np.float32)
    c["c_cos"] = np.ascontiguousarray(np.cos(ang).T.astype(np.float32))
    c["c_sin"] = np.ascontiguousarray(np.sin(ang).T.astype(np.float32))
    m = np.arange(128)[:, None]
    n = np.arange(128)[None, :]
    decT = np.zeros((128, RH, 128), np.float32)
    cdB = np.zeros((128, RH, 128), np.float32)
    sd = np.zeros((128, RH), np.float32)
    for h in range(RH):
        lg = math.log(GAMMA[h])
        decT[:, h, :] = np.where(n >= m, np.exp((n - m) * lg), 0.0) / 16.0
        cdB[:, h, :] = np.exp((np.arange(128) + 1.0) * lg)[None, :]
        sd[:, h] = np.exp((127.0 - np.arange(128)) * lg) / 16.0
    c["c_decT"] = decT
    c["c_cdB"] = cdB
    c["c_sd"] = sd
    return c


def _core_inputs(inp, b, consts):
    f = lambda a: np.ascontiguousarray(np.asarray(a, dtype=np.float32))
    m = {}
    m["xT"] = f(inp["x"][b].T)
    m["cT"] = f(inp["c"][b].reshape(16, 128).T)
    m["w_mod"] = f(inp["w_mod"])
    m["b_modT"] = f(inp["b_mod"].reshape(2, 96, 128).transpose(0, 2, 1))
    m["nmixT"] = f(inp["norm_mix"].reshape(2, 16, 128).transpose(0, 2, 1))
    m["nffnT"] = f(inp["norm_ffn"].reshape(2, 16, 128).transpose(0, 2, 1))
    m["relb"] = f(inp["rel_bias"])
    m["w_qkv"] = f(inp["att_w_qkv"][0])
    m["qgT"] = f(inp["att_q_gain"][0].T)
    m["kgT"] = f(inp["att_k_gain"][0].T)
    m["w_ao"] = f(inp["att_w_o"][0])
    m["w_rq"] = f(inp["ret_w_qkvg"][0])
    m["w_ro"] = f(inp["ret_w_o"][0])
    m["w_up"] = f(inp["ffn_w_up"])
    m["w_dn"] = f(inp["ffn_w_down"])
    m.update(consts)
    return m


def build_nc():
    nc = bass.Bass("TRN2", target_bir_lowering=False)
    P = Prog(nc)
    P.build_all()
    return nc


def kernel(**inputs):
    inp = {k: np.asarray(v) for k, v in inputs.items()}
    nc = build_nc()
    consts = _consts()
    shared = _core_inputs(inp, 0, consts)
    in_maps = []
    for b in range(8):
        m = dict(shared)
        m["xT"] = np.ascontiguousarray(inp["x"][b].T.astype(np.float32))
        m["cT"] = np.ascontiguousarray(inp["c"][b].reshape(16, 128).T.astype(np.float32))
        in_maps.append(m)
    res = run_bass_kernel_spmd(nc, in_maps, core_ids=list(range(8)))
    out = np.stack([np.asarray(r["outT"]).T for r in res.results], axis=0)
    return np.ascontiguousarray(out.astype(np.float32))
```
